# Optimizing a Trainium2 kernel written in Bass

```python
import math
import jax, jax.numpy as jnp
from jax import lax
import numpy as np

D_MODEL = 4096
BATCH = 2
SEQ = 4096
DEPTH = 1
DEC_BATCH = 16
DEC_SEQ = 64
PAST_LEN = 1024

CHUNK = 64
Q_BLOCK = 128
CONV_DIM = D_MODEL
CONV_WIDTH = 31
N_HEADS = 64
Q_RANK = 1024
KV_RANK = 512
NOPE_DIM = 128
ROPE_DIM = 64
V_DIM = 128
QK_DIM = NOPE_DIM + ROPE_DIM
ATTN_DIM = N_HEADS * V_DIM
D_FF = 4 * D_MODEL
ROPE_THETA = 10000.0
ALPHA = (2 * DEPTH) ** 0.25
BETA = (8 * DEPTH) ** -0.25
LN_EPS = 1e-5
RMS_EPS = 1e-6
NEG_INF = -1e30
IN_SIZES = (CONV_DIM, CONV_DIM, Q_RANK, KV_RANK + ROPE_DIM, D_MODEL, D_MODEL)
IN_DIM = sum(IN_SIZES)
IN_SPLIT_POINTS = [int(v) for v in np.cumsum(IN_SIZES)[:-1]]

kernel_name = "streaming_conformer_mla_hybrid_step"


def layer_norm(x, g, b):
    xf = x.astype(jnp.float32)
    mu = xf.mean(-1, keepdims=True)
    var = jnp.square(xf - mu).mean(-1, keepdims=True)
    return ((xf - mu) * lax.rsqrt(var + LN_EPS)).astype(x.dtype) * g + b


def rms_norm(x, g):
    xf = x.astype(jnp.float32)
    return (xf * lax.rsqrt(jnp.square(xf).mean(-1, keepdims=True) + RMS_EPS)).astype(x.dtype) * g


def rope(x, pos):
    half = ROPE_DIM // 2
    inv = ROPE_THETA ** (-jnp.arange(half, dtype=jnp.float32) / half)
    ang = pos.astype(jnp.float32)[:, None] * inv[None, :]
    shape = (1, pos.shape[0]) + (1,) * (x.ndim - 3) + (half,)
    cos = jnp.cos(ang).reshape(shape).astype(x.dtype)
    sin = jnp.sin(ang).reshape(shape).astype(x.dtype)
    x1, x2 = x[..., :half], x[..., half:]
    return jnp.concatenate([x1 * cos - x2 * sin, x1 * sin + x2 * cos], axis=-1)


def in_project(x, w_in, b_in):
    return jnp.split(x @ w_in + b_in, IN_SPLIT_POINTS, axis=-1)


def conv_module(glu_a, glu_b, hist, w_dw, b_dw, cn_g, cn_b, w_pw):
    u = glu_a * jax.nn.sigmoid(glu_b)
    full = jnp.concatenate([hist.astype(u.dtype), u], axis=1)
    y = lax.conv_general_dilated(full, w_dw[:, None, :], window_strides=(1,), padding='VALID',
                                 dimension_numbers=('NWC', 'WIO', 'NWC'),
                                 feature_group_count=CONV_DIM) + b_dw
    y = jax.nn.silu(layer_norm(y, cn_g, cn_b))
    return y @ w_pw, full[:, -(CONV_WIDTH - 1):]


def mla_latents(q_lat, kv_lat, pos, q_a_g, w_q_b, kv_a_g):
    b, t, _ = q_lat.shape
    q = (rms_norm(q_lat, q_a_g) @ w_q_b).reshape(b, t, N_HEADS, QK_DIM)
    q_nope = q[..., :NOPE_DIM]
    q_rope = rope(q[..., NOPE_DIM:], pos)
    ckv = rms_norm(kv_lat[..., :KV_RANK], kv_a_g)
    k_rope = rope(kv_lat[..., KV_RANK:], pos)
    return q_nope, q_rope, ckv, k_rope


def mla_prompt(q_nope, q_rope, ckv, k_rope, w_kv_b, w_o):
    b, s = q_nope.shape[0], q_nope.shape[1]
    kv = (ckv @ w_kv_b).reshape(b, s, N_HEADS, NOPE_DIM + V_DIM)
    k = jnp.concatenate([kv[..., :NOPE_DIM],
                         jnp.broadcast_to(k_rope[:, :, None, :], (b, s, N_HEADS, ROPE_DIM))], axis=-1)
    v = kv[..., NOPE_DIM:]
    q = jnp.concatenate([q_nope, q_rope], axis=-1)
    nb = s // Q_BLOCK
    q_blocks = q.reshape(b, nb, Q_BLOCK, N_HEADS, QK_DIM).transpose(1, 0, 2, 3, 4)
    key_chunk = jnp.arange(s) // CHUNK
    scale = QK_DIM ** -0.5

    def block(args):
        qb, i = args
        q_chunk = (i * Q_BLOCK + jnp.arange(Q_BLOCK)) // CHUNK
        sc = jnp.einsum('bqhd,bkhd->bhqk', qb, k).astype(jnp.float32) * scale
        sc = jnp.where(key_chunk[None, :] <= q_chunk[:, None], sc, NEG_INF)
        p = jax.nn.softmax(sc, axis=-1).astype(v.dtype)
        return jnp.einsum('bhqk,bkhd->bqhd', p, v)

    o = lax.map(block, (q_blocks, jnp.arange(nb)))
    o = o.transpose(1, 0, 2, 3, 4).reshape(b, s, ATTN_DIM)
    return o @ w_o


def mla_sample(q_nope, q_rope, ckv_all, krope_all, w_kv_b, w_o):
    b, t = q_nope.shape[0], q_nope.shape[1]
    w = w_kv_b.reshape(KV_RANK, N_HEADS, NOPE_DIM + V_DIM)
    w_uk, w_uv = w[..., :NOPE_DIM], w[..., NOPE_DIM:]
    q_abs = jnp.einsum('bqhd,rhd->bqhr', q_nope, w_uk)
    sc = (jnp.einsum('bqhr,bkr->bhqk', q_abs, ckv_all)
          + jnp.einsum('bqhd,bkd->bhqk', q_rope, krope_all)).astype(jnp.float32) * (QK_DIM ** -0.5)
    p = jax.nn.softmax(sc, axis=-1).astype(ckv_all.dtype)
    o_lat = jnp.einsum('bhqk,bkr->bqhr', p, ckv_all)
    o = jnp.einsum('bqhr,rhd->bqhd', o_lat, w_uv).reshape(b, t, ATTN_DIM)
    return o @ w_o


def merge_and_ffn(x, conv_out, attn_out, gate_c, gate_a, w_out, ln1_g, ln1_b, w_up, w_down, ln2_g, ln2_b):
    merged = jax.nn.sigmoid(gate_c) * conv_out + jax.nn.sigmoid(gate_a) * attn_out
    h = layer_norm(ALPHA * x + merged @ w_out, ln1_g, ln1_b)
    f = jnp.square(jax.nn.relu(h @ w_up)) @ w_down
    return layer_norm(ALPHA * h + f, ln2_g, ln2_b)


def setup_inputs(seed: int = 0) -> dict:
    key = jax.random.key(seed)
    ks = jax.random.split(key, 32)

    def nrm(k, shape, scale):
        return jax.random.normal(k, shape, jnp.float32) * scale

    L = DEPTH
    return {
        "x_prompt": nrm(ks[0], (BATCH, SEQ, D_MODEL), 1.0),
        "x_sample": nrm(ks[1], (DEC_BATCH, DEC_SEQ, D_MODEL), 1.0),
        "cache_ckv": nrm(ks[2], (L, DEC_BATCH, PAST_LEN, KV_RANK), 1.0),
        "cache_krope": nrm(ks[3], (L, DEC_BATCH, PAST_LEN, ROPE_DIM), 1.0),
        "state_conv": nrm(ks[4], (L, DEC_BATCH, CONV_WIDTH - 1, CONV_DIM), 0.5),
        "w_in": nrm(ks[5], (L, D_MODEL, IN_DIM), D_MODEL ** -0.5),
        "b_in": nrm(ks[6], (L, IN_DIM), 0.01),
        "w_dw": nrm(ks[7], (L, CONV_WIDTH, CONV_DIM), CONV_WIDTH ** -0.5),
        "b_dw": nrm(ks[8], (L, CONV_DIM), 0.01),
        "conv_ln_g": 1.0 + nrm(ks[9], (L, CONV_DIM), 0.01),
        "conv_ln_b": nrm(ks[10], (L, CONV_DIM), 0.01),
        "w_conv_pw": nrm(ks[11], (L, CONV_DIM, D_MODEL), CONV_DIM ** -0.5),
        "q_a_g": 1.0 + nrm(ks[12], (L, Q_RANK), 0.01),
        "w_q_b": nrm(ks[13], (L, Q_RANK, N_HEADS * QK_DIM), Q_RANK ** -0.5),
        "kv_a_g": 1.0 + nrm(ks[14], (L, KV_RANK), 0.01),
        "w_kv_b": nrm(ks[15], (L, KV_RANK, N_HEADS * (NOPE_DIM + V_DIM)), KV_RANK ** -0.5),
        "w_attn_o": nrm(ks[16], (L, ATTN_DIM, D_MODEL), ATTN_DIM ** -0.5),
        "w_out": nrm(ks[17], (L, D_MODEL, D_MODEL), BETA * D_MODEL ** -0.5),
        "ln1_g": 1.0 + nrm(ks[18], (L, D_MODEL), 0.01),
        "ln1_b": nrm(ks[19], (L, D_MODEL), 0.01),
        "w_up": nrm(ks[20], (L, D_MODEL, D_FF), D_MODEL ** -0.5),
        "w_down": nrm(ks[21], (L, D_FF, D_MODEL), BETA * D_FF ** -0.5),
        "ln2_g": 1.0 + nrm(ks[22], (L, D_MODEL), 0.01),
        "ln2_b": nrm(ks[23], (L, D_MODEL), 0.01),
    }


def reference(x_prompt, x_sample, cache_ckv, cache_krope, state_conv, w_in, b_in, w_dw, b_dw,
              conv_ln_g, conv_ln_b, w_conv_pw, q_a_g, w_q_b, kv_a_g, w_kv_b, w_attn_o, w_out,
              ln1_g, ln1_b, w_up, w_down, ln2_g, ln2_b):
    pos_p = jnp.arange(x_prompt.shape[1])
    pos_s = PAST_LEN + jnp.arange(x_sample.shape[1])
    hp, hs = x_prompt, x_sample
    ckv_p_l, kr_p_l, cs_p_l, ckv_s_l, kr_s_l, cs_s_l = [], [], [], [], [], []
    for l in range(DEPTH):
        ga, gb, ql, kl, gc, gat = in_project(hp, w_in[l], b_in[l])
        zero_hist = jnp.zeros((hp.shape[0], CONV_WIDTH - 1, CONV_DIM), hp.dtype)
        conv_p, cs_p = conv_module(ga, gb, zero_hist, w_dw[l], b_dw[l], conv_ln_g[l], conv_ln_b[l], w_conv_pw[l])
        qn, qr, ckv_p, kr_p = mla_latents(ql, kl, pos_p, q_a_g[l], w_q_b[l], kv_a_g[l])
        attn_p = mla_prompt(qn, qr, ckv_p, kr_p, w_kv_b[l], w_attn_o[l])
        hp = merge_and_ffn(hp, conv_p, attn_p, gc, gat, w_out[l], ln1_g[l], ln1_b[l],
                           w_up[l], w_down[l], ln2_g[l], ln2_b[l])
        ga, gb, ql, kl, gc, gat = in_project(hs, w_in[l], b_in[l])
        conv_s, cs_s = conv_module(ga, gb, state_conv[l], w_dw[l], b_dw[l], conv_ln_g[l], conv_ln_b[l], w_conv_pw[l])
        qn, qr, ckv_s, kr_s = mla_latents(ql, kl, pos_s, q_a_g[l], w_q_b[l], kv_a_g[l])
        ckv_all = jnp.concatenate([cache_ckv[l].astype(ckv_s.dtype), ckv_s], axis=1)
        kr_all = jnp.concatenate([cache_krope[l].astype(kr_s.dtype), kr_s], axis=1)
        attn_s = mla_sample(qn, qr, ckv_all, kr_all, w_kv_b[l], w_attn_o[l])
        hs = merge_and_ffn(hs, conv_s, attn_s, gc, gat, w_out[l], ln1_g[l], ln1_b[l],
                           w_up[l], w_down[l], ln2_g[l], ln2_b[l])
        ckv_p_l.append(ckv_p); kr_p_l.append(kr_p); cs_p_l.append(cs_p)
        ckv_s_l.append(ckv_s); kr_s_l.append(kr_s); cs_s_l.append(cs_s)
    return (hp, hs, jnp.stack(ckv_p_l), jnp.stack(kr_p_l), jnp.stack(cs_p_l),
            jnp.stack(ckv_s_l), jnp.stack(kr_s_l), jnp.stack(cs_s_l))
```

```python
import contextlib
import numpy as np
import concourse.bass as bass
import concourse.mybir as mybir
from concourse.bass_utils import run_bass_kernel_spmd

F32 = mybir.dt.float32
BF16 = mybir.dt.bfloat16
ALU = mybir.AluOpType
AF = mybir.ActivationFunctionType

OWN = 1024
NS = 64
T = OWN + 2 * NS
NCTX = 3072
PAST = 1024
NKS = PAST + NS + 64
NK = NCTX + OWN + 2 * NKS
TT = [(0, 512), (512, 512), (1024, 128)]
ROPE = 64
CW = 31
LN_EPS = 1e-5
RMS_EPS = 1e-6
ARENA_WORDS = 52992


class R:
    __slots__ = ("w", "rd", "ps")

    def __init__(self):
        self.w = None
        self.rd = {}
        self.ps = None


class Sched:
    ENGS = ("pe", "act", "dve", "pool", "sp")
    NDMA = 24
    NPOOL = 16

    def __init__(self, nc):
        self.nc = nc
        self.streams = {e: [] for e in self.ENGS}
        self.count = {e: 0 for e in self.ENGS}
        self.seen = {e: {} for e in self.ENGS}
        self.dcount = {("d", i): 0 for i in range(self.NDMA)}
        self.dnext = 0
        self.pcount = {("p", i): 0 for i in range(self.NPOOL)}
        self.pnext = 0

    def recycle(self):
        pass

    def _wait(self, eng, key, val):
        if self.seen[eng].get(key, 0) >= val:
            return
        self.seen[eng][key] = val
        self.streams[eng].append(("w", key, val))

    maxops = None
    nrec = 0
    log = []

    def op(self, eng, fn, reads=(), writes=(), dma=False):
        Sched.nrec += 1
        if Sched.maxops is not None:
            import sys as _sys
            Sched.log.append((Sched.nrec, eng, _sys._getframe(1).f_lineno, _sys._getframe(2).f_lineno))
            if Sched.nrec > Sched.maxops:
                return None
        deps = {}
        for r in reads:
            if r.w is not None:
                k, v, s = r.w
                if deps.get(k, (0, None))[0] < v:
                    deps[k] = (v, s)
        for r in writes:
            if r.w is not None:
                k, v, s = r.w
                if deps.get(k, (0, None))[0] < v:
                    deps[k] = (v, s)
            for k, (v, s) in r.rd.items():
                if deps.get(k, (0, None))[0] < v:
                    deps[k] = (v, s)
        for k, (v, s) in deps.items():
            if s == "pe" and eng == "pe" and not dma:
                continue
            self._wait(eng, k, v)
        if dma and eng == "pool":
            key = ("p", self.pnext)
            self.pnext = (self.pnext + 1) % self.NPOOL
            if self.pcount[key] > 0:
                self._wait(eng, key, self.pcount[key])
            self.pcount[key] += 16
            tok = (key, self.pcount[key], "dma")
            self.streams[eng].append(("o", fn, key, 16))
        elif dma:
            key = ("d", self.dnext)
            self.dnext = (self.dnext + 1) % self.NDMA
            if self.dcount[key] > 0:
                self._wait(eng, key, self.dcount[key])
            self.dcount[key] += 16
            tok = (key, self.dcount[key], "dma")
            self.streams[eng].append(("o", fn, key, 16))
        else:
            self.count[eng] += 1
            tok = (eng, self.count[eng], eng)
            self.streams[eng].append(("o", fn, eng, 1))
        k, v, s = tok
        for r in reads:
            if r.rd.get(k, (0, None))[0] < v:
                r.rd[k] = (v, s)
        for r in writes:
            r.w = tok
            r.rd = {}
        return tok

    def barrier(self):
        for e in self.ENGS:
            for o in self.ENGS:
                if self.count[o] > 0:
                    self._wait(e, o, self.count[o])
            for k, v in self.dcount.items():
                if v > 0:
                    self._wait(e, k, v)
            for k, v in self.pcount.items():
                if v > 0:
                    self._wait(e, k, v)

    def emit(self):
        nc = self.nc
        with contextlib.ExitStack() as es:
            sems = {}
            for e in self.ENGS:
                sems[e] = es.enter_context(nc.semaphore("s_" + e))
            for k in self.dcount:
                sems[k] = es.enter_context(nc.semaphore("s_d%d" % k[1]))
            for k in self.pcount:
                sems[k] = es.enter_context(nc.semaphore("s_p%d" % k[1]))
            self.barrier()
            block = es.enter_context(nc.Block())

            def run(engname):
                def body(eng):
                    for it in self.streams[engname]:
                        if it[0] == "w":
                            eng.wait_ge(sems[it[1]], it[2])
                        elif it[0] == "c":
                            eng.sem_clear(sems[it[1]])
                        else:
                            it[1](eng).then_inc(sems[it[2]], it[3])
                return body

            block.tensor(run("pe"))
            block.scalar(run("act"))
            block.vector(run("dve"))
            block.gpsimd(run("pool"))
            block.sync(run("sp"))


def build(cfg, debug=False):
    D = cfg["D"]; QR = cfg["QR"]; KVR = cfg["KVR"]; H = cfg["H"]; DFF = cfg["DFF"]
    CD = D
    KB = D // 128; CB = CD // 128; QB = QR // 128; RB = KVR // 128
    KLW = KVR + ROPE
    ALPHA = cfg["ALPHA"]
    SCALE = 192.0 ** -0.5
    nc = bass.Bass("TRN2", target_bir_lowering=False)

    def din(name, shape, dt=F32):
        return nc.dram_tensor(name, list(shape), dt, kind="ExternalInput").ap()

    def dout(name, shape, dt=F32):
        return nc.dram_tensor(name, list(shape), dt, kind="ExternalOutput").ap()

    def dscr(name, shape, dt):
        kind = "ExternalOutput" if debug else "Internal"
        return nc.dram_tensor(name, list(shape), dt, kind=kind).ap()

    x_all = din("x_all", [T, D]); x_halo = din("x_halo", [32, D]); x_ctx = din("x_ctx", [NCTX, D])
    cache_ckv = din("cache_ckv", [2 * PAST, KVR]); cache_kr = din("cache_kr", [2 * PAST, ROPE])
    state_conv = din("state_conv", [64, CD])
    w_glu = din("w_glu", [D, 2 * CD]); w_ql = din("w_ql", [D, QR]); w_kl = din("w_kl", [D, KLW])
    w_gate = din("w_gate", [D, 2 * D])
    b_ga = din("b_ga", [128, CB]); b_gb = din("b_gb", [128, CB]); b_ql = din("b_ql", [128, QB])
    b_gate = din("b_gate", [128, 2 * KB]); b_kl = din("b_kl", [1, KLW])
    wdw = din("wdw", [128, CB * CW]); bdw = din("bdw", [128, CB])
    cng = din("cng", [128, CB]); cnb = din("cnb", [128, CB])
    w_pw = din("w_pw", [CD, D])
    qag = din("qag", [128, QB]); w_q = din("w_q", [QR, H * 256])
    kvg = din("kvg", [1, KVR]); w_kv = din("w_kv", [KVR, H * 256])
    w_o = din("w_o", [H * 128, D]); w_out = din("w_out", [D, D])
    ln1g = din("ln1g", [1, D]); ln1b = din("ln1b", [1, D])
    w_up = din("w_up", [D, DFF]); w_down = din("w_down", [DFF, D])
    ln2g = din("ln2g", [1, D]); ln2b = din("ln2b", [1, D])
    halo_valid = din("halo_valid", [128, 1]); keybias = din("keybias", [128, NCTX // 128])
    tm_cos = din("tm_cos", [T + NCTX, 32]); tm_sin = din("tm_sin", [T + NCTX, 32])
    fm_c = din("fm_c", [64, T]); fm_s = din("fm_s", [64, T])

    y_out = dout("y_out", [T, D]); ckv_out = dout("ckv_out", [T, KVR]); kr_out = dout("kr_out", [T, ROPE])
    cs_out = dout("cs_out", [96, CD])

    YT = dscr("YT", [CD, T], BF16); SGT = dscr("SGT", [2 * D, T], BF16); M1T = dscr("M1T", [D, T], F32)
    QNT = dscr("QNT", [QR, T], BF16); CKVT = dscr("CKVT", [KVR, NK], BF16); KRT = dscr("KRT", [64, NK], BF16)
    OT = dscr("OT", [H * 128, T], BF16); MGT = dscr("MGT", [D, T], BF16)
    HP = dscr("HP", [T, D], F32); HS = dscr("HS", [T, D], F32); HT = dscr("HT", [D, T], BF16)

    with contextlib.ExitStack() as es:
        arena = es.enter_context(nc.sbuf_tensor("arena", [128, ARENA_WORDS], F32))
        psum = es.enter_context(nc.psum_tensor("psum", [128, 4096], F32))
        S = Sched(nc)
        state = {"top": 0}

        def alloc_f32(n):
            n = (n + 7) // 8 * 8
            o = state["top"]; state["top"] += n
            assert state["top"] <= ARENA_WORDS, ("arena overflow", state["top"])
            return arena[:, o:o + n]

        def A32(shape):
            n = int(np.prod(shape[1:]))
            ap = alloc_f32(n)[0:shape[0], 0:n]
            if len(shape) == 3:
                ap = ap.rearrange("p (a b) -> p a b", b=shape[2])
            return ap

        def A16(shape):
            n = int(np.prod(shape[1:]))
            assert n % 2 == 0
            ap = alloc_f32(n // 2).bitcast(BF16)[0:shape[0], 0:n]
            if len(shape) == 3:
                ap = ap.rearrange("p (a b) -> p a b", b=shape[2])
            return ap

        banks = [psum[:, b * 512:(b + 1) * 512] for b in range(8)]
        banks16 = [psum[:, b * 512:(b + 1) * 512].bitcast(BF16) for b in range(8)]
        rbank = [R() for _ in range(8)]
        bstate = {"i": 0}

        def next_bank(lo=0, hi=8):
            i = bstate["i"]
            b = lo + i % (hi - lo)
            bstate["i"] = i + 1
            return b

        def dma(eng, out, in_, reads=(), writes=()):
            return S.op(eng, lambda e: e.dma_start(out=out, in_=in_), reads=reads, writes=writes, dma=True)

        cp_state = {"i": 0}

        def evac_copy(out, in_, reads, writes, same=False):
            if not same:
                cp_state["i"] += 1
            mode = cfg.get("evac", "alt")
            if (mode == "alt" and cp_state["i"] % 2) or mode == "act":
                return S.op("act", lambda e: e.activation(out=out, in_=in_, func=AF.Copy), reads=reads, writes=writes)
            return S.op("dve", lambda e: e.tensor_copy(out=out, in_=in_), reads=reads, writes=writes)

        ident_f = A32([128, 128]); ident_b = A16([128, 128]); ones_b = A16([128, 128]); ones_f = A32([128, 128])
        r_const = R()
        S.op("pool", lambda e: e.memset(ident_f, 0.0), writes=[r_const])
        S.op("pool", lambda e: e.affine_select(out=ident_f, in_=ident_f, pattern=[[-1, 128]],
                                               compare_op=ALU.not_equal, fill=1.0, base=0,
                                               channel_multiplier=1), reads=[r_const], writes=[r_const])
        S.op("dve", lambda e: e.tensor_copy(out=ident_b, in_=ident_f), reads=[r_const], writes=[r_const])
        S.op("dve", lambda e: e.memset(ones_b, 1.0), writes=[r_const])
        S.op("dve", lambda e: e.memset(ones_f, 1.0), writes=[r_const])
        PERM_TOP = state["top"]

        def new_stage():
            S.barrier()
            S.recycle()
            state["top"] = PERM_TOP
            for r in rbank:
                r.w = None; r.rd = {}

        def load_small(dst, src):
            r = R()
            dma("sp", dst, src, writes=[r])
            return r

        xT = A16([128, KB, T]); r_xT = [R() for _ in range(T // 128)]
        xTh = A16([128, KB, 32]); r_xTh = R()

        def layer_norm_rows(hp_t, r_hp, gbt, bbt, r_gb, junk_t, r_junk_, stt, r_stt):
            S.op("dve", lambda e: e.memset(stt[:, 0:2], 0.0), writes=[r_stt])
            S.op("act", lambda e: e.activation(out=junk_t, in_=hp_t, func=AF.Identity, accum_out=stt[:, 0:1]), reads=[r_hp], writes=[r_junk_, r_stt])
            S.op("act", lambda e: e.activation(out=junk_t, in_=hp_t, func=AF.Square, accum_out=stt[:, 1:2]), reads=[r_hp], writes=[r_junk_, r_stt])
            S.op("dve", lambda e: e.tensor_scalar(out=stt[:, 2:3], in0=stt[:, 0:1], scalar1=1.0 / D, scalar2=None, op0=ALU.mult), reads=[r_stt], writes=[r_stt])
            S.op("dve", lambda e: e.tensor_tensor(out=stt[:, 3:4], in0=stt[:, 2:3], in1=stt[:, 2:3], op=ALU.mult), reads=[r_stt], writes=[r_stt])
            S.op("dve", lambda e: e.scalar_tensor_tensor(out=stt[:, 4:5], in0=stt[:, 1:2], scalar=1.0 / D, in1=stt[:, 3:4], op0=ALU.mult, op1=ALU.subtract),
                 reads=[r_stt], writes=[r_stt])
            S.op("act", lambda e: e.activation(out=stt[:, 5:6], in_=stt[:, 4:5], func=AF.Sqrt, bias=LN_EPS), reads=[r_stt], writes=[r_stt])
            S.op("dve", lambda e: e.reciprocal(out=stt[:, 6:7], in_=stt[:, 5:6]), reads=[r_stt], writes=[r_stt])
            S.op("dve", lambda e: e.tensor_scalar(out=hp_t, in0=hp_t, scalar1=stt[:, 2:3], scalar2=stt[:, 6:7], op0=ALU.subtract, op1=ALU.mult),
                 reads=[r_stt], writes=[r_hp])
            S.op("dve", lambda e: e.tensor_tensor(out=hp_t, in0=hp_t, in1=gbt, op=ALU.mult), reads=r_gb, writes=[r_hp])
            S.op("dve", lambda e: e.tensor_tensor(out=hp_t, in0=hp_t, in1=bbt, op=ALU.add), reads=r_gb, writes=[r_hp])


        def stage_A():
            wkl = A16([128, KB, KLW]); r_wkl = R()
            bklb = A32([128, KLW]); kvgb = A32([128, KVR])
            r_bkl = load_small(bklb, b_kl[0:1, :].partition_broadcast(128))
            r_kvg = load_small(kvgb, kvg[0:1, :].partition_broadcast(128))
            dma("pool", wkl, w_kl.rearrange("(k p) c -> p k c", p=128), writes=[r_wkl])
            xrow = [A16([128, D]) for _ in range(2)]; r_xrow = [R(), R()]
            xTb = [A16([128, KB, 128]) for _ in range(2)]; r_xTb = [R(), R()]
            klf = [A32([128, KLW]) for _ in range(2)]; r_klf = [R(), R()]
            ckf = [A32([128, KLW]) for _ in range(2)]; r_ckf = [R(), R()]
            cbf = [A16([128, KLW]) for _ in range(2)]; r_cbf = [R(), R()]
            junk = A32([128, KVR]); r_junk = R()
            cs_t = [A32([128, 64]) for _ in range(2)]; r_cs = [R(), R()]
            st_t = [A32([128, 8]) for _ in range(2)]; r_st = [R(), R()]
            rt_t = [A32([128, 128]) for _ in range(2)]; r_rt = [R(), R()]
            ckst = [A16([128, RB, 128]) for _ in range(2)]; r_ckst = [R(), R()]
            krst = [A16([64, 128]) for _ in range(2)]; r_krst = [R(), R()]
            r_scrA = R()

            hrow = A16([32, D]); r_hrow = R()
            dma("pool", hrow, x_halo[:, :], writes=[r_hrow])
            for g in range(0, KB, 8):
                b = next_bank(); ng = min(8, KB - g)
                for k in range(g, g + ng):
                    S.op("pe", lambda e, b=b, k=k, g=g: e.transpose(banks16[b][:, (k - g) * 32:(k - g) * 32 + 32],
                                                                   hrow[:, k * 128:(k + 1) * 128], ident_b[0:32, 0:32]),
                         reads=[r_hrow, r_const], writes=[rbank[b]])
                evac_copy(xTh[:, g:g + ng, :], banks16[b][:, 0:ng * 32].rearrange("p (k n) -> p k n", n=32),
                          reads=[rbank[b]], writes=[r_xTh])

            def latent_tail(i, key_cols, out_rows):
                b = next_bank()
                for r in range(RB):
                    S.op("pe", lambda e, b=b, r=r: e.transpose(banks16[b][:, r * 128:(r + 1) * 128],
                                                               cbf[i][:, r * 128:(r + 1) * 128], ident_b),
                         reads=[r_cbf[i], r_const], writes=[rbank[b]])
                S.op("pe", lambda e, b=b: e.transpose(banks16[b][0:64, RB * 128:(RB + 1) * 128],
                                                      cbf[i][:, KVR:KLW], ident_b),
                     reads=[r_cbf[i], r_const], writes=[rbank[b]])
                evac_copy(ckst[i], banks16[b][:, 0:RB * 128].rearrange("p (k n) -> p k n", n=128),
                          reads=[rbank[b]], writes=[r_ckst[i]])
                evac_copy(krst[i], banks16[b][0:64, RB * 128:(RB + 1) * 128], reads=[rbank[b]], writes=[r_krst[i]], same=True)
                for (c0, n, k0) in (key_cols if "s" not in cfg.get("skip", "") else []):
                    dma("sp", CKVT[:, k0:k0 + n].rearrange("(r p) n -> p r n", p=128), ckst[i][:, :, c0:c0 + n],
                        reads=[r_ckst[i]], writes=[r_scrA])
                    dma("sp", KRT[:, k0:k0 + n], krst[i][:, c0:c0 + n], reads=[r_krst[i]], writes=[r_scrA])

            zpad = A16([128, RB, 64]); r_zpad = R()
            S.op("dve", lambda e: e.memset(zpad, 0.0), writes=[r_zpad])
            for s_ in range(2):
                kz = NCTX + OWN + s_ * NKS + PAST + NS
                dma("sp", CKVT[:, kz:kz + 64].rearrange("(r p) n -> p r n", p=128), zpad, reads=[r_zpad], writes=[r_scrA])
                dma("sp", KRT[:, kz:kz + 64], zpad[0:64, 0, :], reads=[r_zpad], writes=[r_scrA])
            blocks = []
            for i in range(OWN // 128):
                blocks.append(("x", x_all[i * 128:(i + 1) * 128, :], i, i * 128, [(0, 128, NCTX + i * 128)], i * 128))
            blocks.append(("x", x_all[OWN:T, :], OWN // 128, OWN, [(0, 64, NCTX + OWN + PAST), (64, 64, NCTX + OWN + NKS + PAST)], OWN))
            for i in range(NCTX // 128):
                blocks.append(("c", x_ctx[i * 128:(i + 1) * 128, :], None, T + i * 128, [(0, 128, i * 128)], None))
            for s in range(2):
                for i in range(PAST // 128):
                    blocks.append(("k", s * PAST + i * 128, None, None, [(0, 128, NCTX + OWN + s * NKS + i * 128)], None))

            for bi, blk in list(enumerate(blocks))[cfg.get("blk0", 0):cfg.get("nblk", 10 ** 6)]:
                i = bi % 2
                kind = blk[0]
                if cfg.get("blkbar"):
                    S.barrier()
                if kind == "k":
                    r0 = blk[1]
                    dma("pool", cbf[i][:, 0:KVR], cache_ckv[r0:r0 + 128, :], writes=[r_cbf[i]])
                    dma("pool", cbf[i][:, KVR:KLW], cache_kr[r0:r0 + 128, :], writes=[r_cbf[i]])
                    latent_tail(i, blk[4], None)
                    continue
                _, src, tb, trow, key_cols, orow = blk
                dma("pool", xrow[i], src, writes=[r_xrow[i]])
                if kind == "x":
                    dstT = xT[:, :, tb * 128:(tb + 1) * 128]; r_dst = r_xT[tb]
                else:
                    dstT = xTb[i]; r_dst = r_xTb[i]
                for g in range(0, KB, 8):
                    b = next_bank(); ng = min(8, KB - g)
                    for k in range(g, g + ng):
                        S.op("pe", lambda e, b=b, k=k, g=g, i=i: e.transpose(banks16[b][:, (k - g) * 128:(k - g + 1) * 128],
                                                                            xrow[i][:, k * 128:(k + 1) * 128], ident_b),
                             reads=[r_xrow[i], r_const], writes=[rbank[b]])
                    evac_copy(dstT[:, g:g + ng, :], banks16[b][:, 0:ng * 128].rearrange("p (k n) -> p k n", n=128),
                              reads=[rbank[b]], writes=[r_dst])
                bA = next_bank(); bB = next_bank()
                for k in range(KB):
                    S.op("pe", lambda e, k=k, bA=bA, dstT=dstT: e.matmul(banks[bA][:, 0:KVR], lhsT=dstT[:, k, :], rhs=wkl[:, k, 0:KVR],
                                                                        start=(k == 0), stop=(k == KB - 1)),
                         reads=[r_dst, r_wkl], writes=[rbank[bA]])
                for k in range(KB):
                    S.op("pe", lambda e, k=k, bB=bB, dstT=dstT: e.matmul(banks[bB][:, 0:ROPE], lhsT=dstT[:, k, :], rhs=wkl[:, k, KVR:KLW],
                                                                        start=(k == 0), stop=(k == KB - 1)),
                         reads=[r_dst, r_wkl], writes=[rbank[bB]])
                S.op("dve", lambda e, i=i, bA=bA: e.tensor_tensor(out=klf[i][:, 0:KVR], in0=banks[bA][:, 0:KVR], in1=bklb[:, 0:KVR], op=ALU.add),
                     reads=[rbank[bA], r_bkl], writes=[r_klf[i]])
                S.op("dve", lambda e, i=i, bB=bB: e.tensor_tensor(out=klf[i][:, KVR:KLW], in0=banks[bB][:, 0:ROPE], in1=bklb[:, KVR:KLW], op=ALU.add),
                     reads=[rbank[bB], r_bkl], writes=[r_klf[i]])
                dma("sp", cs_t[i][:, 0:32], tm_cos[trow:trow + 128, :], writes=[r_cs[i]])
                dma("sp", cs_t[i][:, 32:64], tm_sin[trow:trow + 128, :], writes=[r_cs[i]])
                S.op("dve", lambda e, i=i: e.memset(st_t[i][:, 0:1], 0.0), writes=[r_st[i]])
                S.op("act", lambda e, i=i: e.activation(out=junk, in_=klf[i][:, 0:KVR], func=AF.Square, accum_out=st_t[i][:, 0:1]),
                     reads=[r_klf[i]], writes=[r_junk, r_st[i]])
                S.op("act", lambda e, i=i: e.activation(out=st_t[i][:, 1:2], in_=st_t[i][:, 0:1], func=AF.Sqrt, scale=1.0 / KVR, bias=RMS_EPS),
                     reads=[r_st[i]], writes=[r_st[i]])
                S.op("dve", lambda e, i=i: e.reciprocal(out=st_t[i][:, 2:3], in_=st_t[i][:, 1:2]), reads=[r_st[i]], writes=[r_st[i]])
                S.op("dve", lambda e, i=i: e.scalar_tensor_tensor(out=ckf[i][:, 0:KVR], in0=klf[i][:, 0:KVR], scalar=st_t[i][:, 2:3],
                                                                  in1=kvgb, op0=ALU.mult, op1=ALU.mult),
                     reads=[r_klf[i], r_st[i], r_kvg], writes=[r_ckf[i]])
                x1 = klf[i][:, KVR:KVR + 32]; x2 = klf[i][:, KVR + 32:KLW]
                co = cs_t[i][:, 0:32]; si = cs_t[i][:, 32:64]
                t = rt_t[i]
                S.op("dve", lambda e, x1=x1, co=co, t=t: e.tensor_tensor(out=t[:, 0:32], in0=x1, in1=co, op=ALU.mult), reads=[r_klf[i], r_cs[i]], writes=[r_rt[i]])
                S.op("dve", lambda e, x2=x2, si=si, t=t: e.tensor_tensor(out=t[:, 32:64], in0=x2, in1=si, op=ALU.mult), reads=[r_klf[i], r_cs[i]], writes=[r_rt[i]])
                S.op("dve", lambda e, x1=x1, si=si, t=t: e.tensor_tensor(out=t[:, 64:96], in0=x1, in1=si, op=ALU.mult), reads=[r_klf[i], r_cs[i]], writes=[r_rt[i]])
                S.op("dve", lambda e, x2=x2, co=co, t=t: e.tensor_tensor(out=t[:, 96:128], in0=x2, in1=co, op=ALU.mult), reads=[r_klf[i], r_cs[i]], writes=[r_rt[i]])
                S.op("dve", lambda e, i=i, t=t: e.tensor_tensor(out=ckf[i][:, KVR:KVR + 32], in0=t[:, 0:32], in1=t[:, 32:64], op=ALU.subtract), reads=[r_rt[i]], writes=[r_ckf[i]])
                S.op("dve", lambda e, i=i, t=t: e.tensor_tensor(out=ckf[i][:, KVR + 32:KLW], in0=t[:, 64:96], in1=t[:, 96:128], op=ALU.add), reads=[r_rt[i]], writes=[r_ckf[i]])
                S.op("act", lambda e, i=i: e.activation(out=cbf[i], in_=ckf[i], func=AF.Copy), reads=[r_ckf[i]], writes=[r_cbf[i]])
                if orow is not None and "o" not in cfg.get("skip", ""):
                    dma("sp", ckv_out[orow:orow + 128, :], ckf[i][:, 0:KVR], reads=[r_ckf[i]])
                    dma("sp", kr_out[orow:orow + 128, :], ckf[i][:, KVR:KLW], reads=[r_ckf[i]])
                latent_tail(i, key_cols, orow)

        if "A" in cfg.get("stages", "ABCDEFG"):
            stage_A()

        def stage_B():
            S.barrier()
            state["top"] = PERM_TOP + (KB * T) // 2 + (KB * 32) // 2 + 16
            for r in rbank:
                r.w = None; r.rd = {}
            TOPB = state["top"]
            bga = A32([128, CB]); bgb = A32([128, CB]); wdwt = A32([128, CB, CW]); bdwt = A32([128, CB])
            hval = A32([128, 1])
            r_p = [load_small(bga, b_ga[:, :]), load_small(bgb, b_gb[:, :]),
                   load_small(wdwt, wdw.rearrange("p (c k) -> p c k", k=CW)), load_small(bdwt, bdw[:, :]),
                   load_small(hval, halo_valid[:, :])]
            histT = A32([128, CB, 64]); r_hist = R()
            tailsT = A32([128, CB, 96]); r_tails = R()
            S.op("dve", lambda e: e.memset(tailsT, 0.0), writes=[r_tails])
            pan = [A16([128, KB, 512]) for _ in range(2)]; r_pan = [R(), R()]
            TOPB1 = state["top"]
            hrowf = A32([64, CD]); r_hrf = R()
            dma("sp", hrowf, state_conv[:, :], writes=[r_hrf])
            for g in range(0, CB, 8):
                b = next_bank(); ng = min(8, CB - g)
                for c in range(g, g + ng):
                    S.op("pe", lambda e, b=b, c=c, g=g: e.transpose(banks[b][:, (c - g) * 64:(c - g + 1) * 64],
                                                                   hrowf[:, c * 128:(c + 1) * 128], ident_f[0:64, 0:64]),
                         reads=[r_hrf, r_const], writes=[rbank[b]])
                evac_copy(histT[:, g:g + ng, :], banks[b][:, 0:ng * 64].rearrange("p (k n) -> p k n", n=64),
                          reads=[rbank[b]], writes=[r_hist])
            S.barrier()
            state["top"] = TOPB1
            full = [A32([128, 1248]) for _ in range(2)]; r_full = [R(), R()]
            sigt = [A32([128, 512]) for _ in range(2)]; r_sig = [R(), R()]
            acc = [A32([128, T]) for _ in range(2)]; r_acc = [R(), R()]
            acc2 = [A32([128, T]) for _ in range(2)]; r_acc2 = [R(), R()]
            ctmp = [A32([128, T])]; r_ctmp = [R()]
            KD = cfg.get("KD", 21)
            ybf = [A16([128, T]) for _ in range(2)]; r_ybf = [R(), R()]
            S.op("dve", lambda e: e.memset(full[0], 0.0), writes=[r_full[0]])
            S.op("dve", lambda e: e.memset(full[1], 0.0), writes=[r_full[1]])
            r_scrB = R()
            xT_all_r = r_xT + [r_xTh]
            npan = (2 * CD) // 512
            dma("pool", pan[0], w_glu[:, 0:512].rearrange("(k p) c -> p k c", p=128), writes=[r_pan[0]])
            for pi in range(npan):
                pb = pi % 2
                if pi + 1 < npan:
                    dma("pool", pan[(pi + 1) % 2], w_glu[:, (pi + 1) * 512:(pi + 2) * 512].rearrange("(k p) c -> p k c", p=128),
                        writes=[r_pan[(pi + 1) % 2]])
                for j in range(2):
                    cb = 2 * pi + j; fb = cb % 2
                    fl = full[fb]
                    for ti, (t0, n) in enumerate(TT):
                        bA = next_bank(); bB = next_bank()
                        groups = [(0, n, xT[:, :, t0:t0 + n])]
                        if ti == 2:
                            groups.append((128, 32, xTh))
                        for (c0, w, rhs) in groups:
                            for (bk, off) in ((bA, j * 128), (bB, 256 + j * 128)):
                                for k in range(KB):
                                    S.op("pe", lambda e, bk=bk, off=off, k=k, c0=c0, w=w, rhs=rhs, pb=pb:
                                         e.matmul(banks[bk][:, c0:c0 + w], lhsT=pan[pb][:, k, off:off + 128], rhs=rhs[:, k, :],
                                                  start=(k == 0), stop=(k == KB - 1)),
                                         reads=[r_pan[pb]] + xT_all_r, writes=[rbank[bk]])
                        wtot = n if ti < 2 else 160
                        S.op("act", lambda e, bB=bB, fb=fb, wtot=wtot, cb=cb: e.activation(out=sigt[fb][:, 0:wtot], in_=banks[bB][:, 0:wtot],
                                                                                     func=AF.Sigmoid, bias=bgb[:, cb:cb + 1]),
                             reads=[rbank[bB]] + r_p, writes=[r_sig[fb]])
                        if ti < 2:
                            dsts = [(fl[:, 32 + t0:32 + t0 + n], 0, n)]
                        else:
                            dsts = [(fl[:, 1088:1152], 0, 64), (fl[:, 1184:1248], 64, 64), (fl[:, 0:32], 128, 32)]
                        for (dst, c0, w) in dsts:
                            S.op("dve", lambda e, dst=dst, c0=c0, w=w, bA=bA, fb=fb, cb=cb:
                                 e.scalar_tensor_tensor(out=dst, in0=banks[bA][:, c0:c0 + w], scalar=bga[:, cb:cb + 1],
                                                        in1=sigt[fb][:, c0:c0 + w], op0=ALU.add, op1=ALU.mult),
                                 reads=[rbank[bA], r_sig[fb]] + r_p, writes=[r_full[fb]])
                        if ti == 2:
                            S.op("dve", lambda e, fl=fl: e.tensor_scalar(out=fl[:, 0:32], in0=fl[:, 0:32], scalar1=hval[:, 0:1], scalar2=None, op0=ALU.mult),
                                 reads=r_p, writes=[r_full[fb]])
                    S.op("act", lambda e, fl=fl, cb=cb: e.activation(out=fl[:, 1058:1088], in_=histT[:, cb, 0:30], func=AF.Copy), reads=[r_hist], writes=[r_full[fb]])
                    S.op("act", lambda e, fl=fl, cb=cb: e.activation(out=fl[:, 1154:1184], in_=histT[:, cb, 32:62], func=AF.Copy), reads=[r_hist], writes=[r_full[fb]])
                    ac = acc[fb]; ac2 = acc2[fb]
                    segs = [(ac[:, 0:OWN], ac2[:, 0:OWN], lambda k, fl=fl: fl[:, 2 + k:2 + k + OWN]),
                            (ac[:, OWN:T].rearrange("p (s n) -> p s n", n=NS), ac2[:, OWN:T].rearrange("p (s n) -> p s n", n=NS),
                             lambda k, fl=fl: fl[:, 1056:1248].rearrange("p (s n) -> p s n", n=96)[:, :, 2 + k:2 + k + NS])]
                    for (oseg, o2seg, inf) in segs:
                        S.op("dve", lambda e, oseg=oseg, inf=inf, cb=cb: e.tensor_scalar(out=oseg, in0=inf(0), scalar1=wdwt[:, cb, 0:1], scalar2=bdwt[:, cb:cb + 1],
                                                                                     op0=ALU.mult, op1=ALU.add),
                             reads=[r_full[fb]] + r_p, writes=[r_acc[fb]])
                        for k in range(1, KD):
                            S.op("dve", lambda e, oseg=oseg, inf=inf, cb=cb, k=k: e.scalar_tensor_tensor(out=oseg, in0=inf(k), scalar=wdwt[:, cb, k:k + 1], in1=oseg,
                                                                                                   op0=ALU.mult, op1=ALU.add),
                                 reads=[r_full[fb]], writes=[r_acc[fb]])
                        S.op("act", lambda e, o2seg=o2seg, inf=inf, cb=cb: e.activation(out=o2seg, in_=inf(KD), func=AF.Identity, scale=wdwt[:, cb, KD:KD + 1]),
                             reads=[r_full[fb]] + r_p, writes=[r_acc2[fb]])
                        for k in range(KD + 1, CW):
                            tq = 0
                            tseg = ctmp[tq][:, 0:OWN] if oseg is segs[0][0] else ctmp[tq][:, OWN:T].rearrange("p (s n) -> p s n", n=NS)
                            S.op("act", lambda e, tseg=tseg, inf=inf, cb=cb, k=k: e.activation(out=tseg, in_=inf(k), func=AF.Identity, scale=wdwt[:, cb, k:k + 1]),
                                 reads=[r_full[fb]] + r_p, writes=[r_ctmp[tq]])
                            S.op("pool", lambda e, o2seg=o2seg, tseg=tseg: e.tensor_tensor(out=o2seg, in0=o2seg, in1=tseg, op=ALU.add),
                                 reads=[r_ctmp[tq]], writes=[r_acc2[fb]])
                        S.op("dve", lambda e, oseg=oseg, o2seg=o2seg: e.tensor_tensor(out=oseg, in0=oseg, in1=o2seg, op=ALU.add),
                             reads=[r_acc2[fb]], writes=[r_acc[fb]])
                    for (d0, s0) in ((0, 1026), (32, 1122), (64, 1218)):
                        S.op("act", lambda e, d0=d0, s0=s0, fl=fl, cb=cb: e.activation(out=tailsT[:, cb, d0:d0 + 30], in_=fl[:, s0:s0 + 30], func=AF.Copy),
                             reads=[r_full[fb]], writes=[r_tails])
                    S.op("act", lambda e, fb=fb: e.activation(out=ybf[fb], in_=acc[fb], func=AF.Copy), reads=[r_acc[fb]], writes=[r_ybf[fb]])
                    dma("sp", YT[cb * 128:(cb + 1) * 128, :], ybf[fb], reads=[r_ybf[fb]], writes=[r_scrB])
            csst = [A32([96, 512])] * 2; r_csst = [R()] * 2
            for g in range(0, CB, 4):
                b = next_bank(); ng = min(4, CB - g); q = (g // 4) % 2
                for c in range(g, g + ng):
                    S.op("pe", lambda e, b=b, c=c, g=g: e.transpose(banks[b][0:96, (c - g) * 128:(c - g + 1) * 128], tailsT[:, c, :], ident_f),
                         reads=[r_tails, r_const], writes=[rbank[b]])
                evac_copy(csst[q][:, 0:ng * 128], banks[b][0:96, 0:ng * 128], reads=[rbank[b]], writes=[r_csst[q]])
                dma("sp", cs_out[:, g * 128:(g + ng) * 128], csst[q][:, 0:ng * 128], reads=[r_csst[q]])

            S.barrier()
            state["top"] = TOPB1
            bqlt = A32([128, QB]); qagt = A32([128, QB])
            r_p2 = [load_small(bqlt, b_ql[:, :]), load_small(qagt, qag[:, :])]
            qlT = A16([128, QB, T]); r_ql = [R() for _ in range(QB)]
            sqt = [A16([128, T]) for _ in range(2)]; r_sq = [R(), R()]
            rstd = A32([128, T]); r_rstd = R()
            rs_t = A32([128, 512]); r_rs = R()
            for pi in range((QR + 511) // 512):
                pb = pi % 2; pw = min(512, QR - pi * 512)
                dma("pool", pan[pb][:, :, 0:pw], w_ql[:, pi * 512:pi * 512 + pw].rearrange("(k p) c -> p k c", p=128), writes=[r_pan[pb]])
                for j in range(pw // 128):
                    mb = pi * 4 + j
                    for ti, (t0, n) in enumerate(TT):
                        b = next_bank(0, 5)
                        for k in range(KB):
                            S.op("pe", lambda e, b=b, k=k, j=j, t0=t0, n=n, pb=pb: e.matmul(banks[b][:, 0:n], lhsT=pan[pb][:, k, j * 128:(j + 1) * 128],
                                                                                      rhs=xT[:, k, t0:t0 + n], start=(k == 0), stop=(k == KB - 1)),
                                 reads=[r_pan[pb]] + r_xT, writes=[rbank[b]])
                        S.op("act", lambda e, b=b, mb=mb, t0=t0, n=n: e.activation(out=qlT[:, mb, t0:t0 + n], in_=banks[b][:, 0:n], func=AF.Identity,
                                                                             bias=bqlt[:, mb:mb + 1]),
                             reads=[rbank[b]] + r_p2, writes=[r_ql[mb]])
            for mb in range(QB):
                q = mb % 2
                S.op("act", lambda e, mb=mb, q=q: e.activation(out=sqt[q], in_=qlT[:, mb, :], func=AF.Square), reads=[r_ql[mb]], writes=[r_sq[q]])
                for ti, (t0, n) in enumerate(TT):
                    S.op("pe", lambda e, ti=ti, t0=t0, n=n, q=q, mb=mb: e.matmul(banks[5 + ti][:, 0:n], lhsT=ones_b, rhs=sqt[q][:, t0:t0 + n],
                                                                           start=(mb == 0), stop=(mb == QB - 1)),
                         reads=[r_sq[q], r_const], writes=[rbank[5 + ti]])
            for ti, (t0, n) in enumerate(TT):
                S.op("act", lambda e, ti=ti, n=n: e.activation(out=rs_t[:, 0:n], in_=banks[5 + ti][:, 0:n], func=AF.Sqrt, scale=1.0 / QR, bias=RMS_EPS),
                     reads=[rbank[5 + ti]], writes=[r_rs])
                S.op("dve", lambda e, t0=t0, n=n: e.reciprocal(out=rstd[:, t0:t0 + n], in_=rs_t[:, 0:n]), reads=[r_rs], writes=[r_rstd])
            for mb in range(QB):
                S.op("dve", lambda e, mb=mb: e.scalar_tensor_tensor(out=qlT[:, mb, :], in0=qlT[:, mb, :], scalar=qagt[:, mb:mb + 1], in1=rstd,
                                                                    op0=ALU.mult, op1=ALU.mult),
                     reads=[r_rstd] + r_p2, writes=[r_ql[mb]])
            dma("sp", QNT.rearrange("(k p) t -> p k t", p=128), qlT, reads=r_ql, writes=[r_scrB])

            S.barrier()
            state["top"] = TOPB1
            for r in rbank:
                r.w = None; r.rd = {}
            bgt = A32([128, 2 * KB]); r_p4 = [load_small(bgt, b_gate[:, :])]
            sgst = [A16([128, T]) for _ in range(2)]; r_sgst = [R(), R()]
            for pi in range((2 * D) // 512):
                pb = pi % 2
                dma("pool", pan[pb], w_gate[:, pi * 512:(pi + 1) * 512].rearrange("(k p) c -> p k c", p=128), writes=[r_pan[pb]])
                for j in range(4):
                    mb = pi * 4 + j; q = mb % 2
                    for ti, (t0, n) in enumerate(TT):
                        b = next_bank()
                        for k in range(KB):
                            S.op("pe", lambda e, b=b, k=k, j=j, t0=t0, n=n, pb=pb: e.matmul(banks[b][:, 0:n], lhsT=pan[pb][:, k, j * 128:(j + 1) * 128],
                                                                                      rhs=xT[:, k, t0:t0 + n], start=(k == 0), stop=(k == KB - 1)),
                                 reads=[r_pan[pb]] + r_xT, writes=[rbank[b]])
                        S.op("act", lambda e, b=b, mb=mb, t0=t0, n=n, q=q: e.activation(out=sgst[q][:, t0:t0 + n], in_=banks[b][:, 0:n], func=AF.Sigmoid,
                                                                                  bias=bgt[:, mb:mb + 1]),
                             reads=[rbank[b]] + r_p4, writes=[r_sgst[q]])
                    dma("sp", SGT[mb * 128:(mb + 1) * 128, :], sgst[q], reads=[r_sgst[q]], writes=[r_scrB])

        if "B" in cfg.get("stages", "ABCDEFG"):
            stage_B()

        def stage_C():
            new_stage()
            cngt = A32([128, CB]); cnbt = A32([128, CB])
            r_pc = [load_small(cngt, cng[:, :]), load_small(cnbt, cnb[:, :])]
            zT = A16([128, CB, T]); r_z = [R() for _ in range(CB)]
            for c in range(CB):
                dma("sp", zT[:, c, :], YT[c * 128:(c + 1) * 128, :], writes=[r_z[c]])
            pan = [A16([128, KB, 512]) for _ in range(2)]; r_pan = [R(), R()]
            sqt = [A16([128, T]) for _ in range(2)]; r_sq = [R(), R()]
            mean = A32([128, T]); rstd = A32([128, T]); nmr = A32([128, T]); r_stat = R()
            tmpc = [A32([128, T]) for _ in range(2)]; r_tmpc = [R(), R()]
            for c in range(CB):
                q = c % 2
                for ti, (t0, n) in enumerate(TT):
                    S.op("pe", lambda e, ti=ti, t0=t0, n=n, c=c: e.matmul(banks[ti][:, 0:n], lhsT=ones_b, rhs=zT[:, c, t0:t0 + n],
                                                                      start=(c == 0), stop=(c == CB - 1)),
                         reads=[r_z[c], r_const], writes=[rbank[ti]])
                S.op("act", lambda e, c=c, q=q: e.activation(out=sqt[q], in_=zT[:, c, :], func=AF.Square), reads=[r_z[c]], writes=[r_sq[q]])
                for ti, (t0, n) in enumerate(TT):
                    S.op("pe", lambda e, ti=ti, t0=t0, n=n, c=c, q=q: e.matmul(banks[3 + ti][:, 0:n], lhsT=ones_b, rhs=sqt[q][:, t0:t0 + n],
                                                                           start=(c == 0), stop=(c == CB - 1)),
                         reads=[r_sq[q], r_const], writes=[rbank[3 + ti]])
            for ti, (t0, n) in enumerate(TT):
                sl = slice(t0, t0 + n)
                S.op("act", lambda e, ti=ti, n=n, sl=sl: e.activation(out=mean[:, sl], in_=banks[ti][:, 0:n], func=AF.Identity, scale=1.0 / CD),
                     reads=[rbank[ti]], writes=[r_stat])
                S.op("dve", lambda e, sl=sl: e.tensor_tensor(out=nmr[:, sl], in0=mean[:, sl], in1=mean[:, sl], op=ALU.mult), reads=[r_stat], writes=[r_stat])
                S.op("dve", lambda e, ti=ti, n=n, sl=sl: e.scalar_tensor_tensor(out=rstd[:, sl], in0=banks[3 + ti][:, 0:n], scalar=1.0 / CD, in1=nmr[:, sl],
                                                                            op0=ALU.mult, op1=ALU.subtract),
                     reads=[rbank[3 + ti], r_stat], writes=[r_stat])
                S.op("act", lambda e, sl=sl: e.activation(out=rstd[:, sl], in_=rstd[:, sl], func=AF.Sqrt, bias=LN_EPS), reads=[r_stat], writes=[r_stat])
                S.op("dve", lambda e, sl=sl: e.reciprocal(out=rstd[:, sl], in_=rstd[:, sl]), reads=[r_stat], writes=[r_stat])
                S.op("dve", lambda e, sl=sl: e.scalar_tensor_tensor(out=nmr[:, sl], in0=mean[:, sl], scalar=-1.0, in1=rstd[:, sl], op0=ALU.mult, op1=ALU.mult),
                     reads=[r_stat], writes=[r_stat])
            for c in range(CB):
                q = c % 2
                S.op("dve", lambda e, c=c, q=q: e.tensor_tensor(out=tmpc[q], in0=zT[:, c, :], in1=rstd, op=ALU.mult), reads=[r_z[c], r_stat], writes=[r_tmpc[q]])
                S.op("dve", lambda e, q=q: e.tensor_tensor(out=tmpc[q], in0=tmpc[q], in1=nmr, op=ALU.add), reads=[r_stat], writes=[r_tmpc[q]])
                S.op("act", lambda e, c=c, q=q: e.activation(out=zT[:, c, :], in_=tmpc[q], func=AF.Silu, scale=cngt[:, c:c + 1], bias=cnbt[:, c:c + 1]),
                     reads=[r_tmpc[q]] + r_pc, writes=[r_z[c]])
            sgb = [A16([128, T]) for _ in range(2)]; r_sgb = [R(), R()]
            m1st = [A32([128, T]) for _ in range(2)]; r_m1st = [R(), R()]
            r_scrC = R()
            for pi in range(D // 512):
                pb = pi % 2
                dma("pool", pan[pb], w_pw[:, pi * 512:(pi + 1) * 512].rearrange("(k p) c -> p k c", p=128), writes=[r_pan[pb]])
                for j in range(4):
                    mb = pi * 4 + j; q = mb % 2
                    dma("sp", sgb[q], SGT[mb * 128:(mb + 1) * 128, :], writes=[r_sgb[q]])
                    for ti, (t0, n) in enumerate(TT):
                        b = 6 + next_bank(0, 2)
                        for k in range(CB):
                            S.op("pe", lambda e, b=b, k=k, j=j, t0=t0, n=n, pb=pb: e.matmul(banks[b][:, 0:n], lhsT=pan[pb][:, k, j * 128:(j + 1) * 128],
                                                                                      rhs=zT[:, k, t0:t0 + n], start=(k == 0), stop=(k == CB - 1)),
                                 reads=[r_pan[pb]] + r_z, writes=[rbank[b]])
                        S.op("dve", lambda e, b=b, t0=t0, n=n, q=q: e.tensor_tensor(out=m1st[q][:, t0:t0 + n], in0=banks[b][:, 0:n], in1=sgb[q][:, t0:t0 + n], op=ALU.mult),
                             reads=[rbank[b], r_sgb[q]], writes=[r_m1st[q]])
                    dma("sp", M1T[mb * 128:(mb + 1) * 128, :], m1st[q], reads=[r_m1st[q]], writes=[r_scrC])

        if "C" in cfg.get("stages", "ABCDEFG"):
            stage_C()

        def stage_D():
            new_stage()
            NKB = NK // 128
            qnT = A16([128, QB, T]); r_qn = R()
            dma("sp", qnT, QNT.rearrange("(k p) t -> p k t", p=128), writes=[r_qn])
            ckvT = A16([128, RB, NK]); r_ckv = R()
            for r in range(RB):
                dma("sp", ckvT[:, r, :], CKVT[r * 128:(r + 1) * 128, :], writes=[r_ckv])
            krT = A16([64, NK]); r_kr = R()
            dma("sp", krT, KRT[:, :], writes=[r_kr])
            fct = A32([64, T]); fst = A32([64, T]); kbt = A32([128, NCTX // 128])
            r_pd = [load_small(fct, fm_c[:, :]), load_small(fst, fm_s[:, :]), load_small(kbt, keybias[:, :])]
            wq = [A16([128, QB, 256]) for _ in range(2)]; r_wq = [R(), R()]
            wkv = [A16([128, RB, 256]) for _ in range(2)]; r_wkv = [R(), R()]
            Qn = A16([128, T]); r_Qn = R()
            Qr = A16([64, T]); r_Qr = R()
            Kh = A16([128, NK]); r_Kh = R()
            Vh = A16([128, NKB, 128]); r_Vh = R()
            t1 = [A32([64, 512]) for _ in range(2)]; r_t1 = [R(), R()]
            t2 = [A32([64, 512]) for _ in range(2)]; r_t2 = [R(), R()]
            Pb = [A16([128, 512]) for _ in range(3)]; r_Pb = [R() for _ in range(3)]
            Pd = [A16([128, 512]) for _ in range(4)]; r_Pd = [R() for _ in range(4)]
            for d in range(4):
                S.op("dve", lambda e, d=d: e.memset(Pd[d], 0.0), writes=[r_Pd[d]])
            rden = [A32([128, 512]) for _ in range(2)]; r_rden = [R(), R()]
            dacc = [A32([128, 512]) for _ in range(2)]; r_dacc = [R(), R()]
            Ost = [A16([128, T]) for _ in range(2)]; r_Ost = [R(), R()]
            r_scrD = R()
            pstate = {"p": 0, "u": 0}
            for h in range(H):
                hb = h % 2
                dma("pool", wq[hb], w_q[:, h * 256:(h + 1) * 256].rearrange("(k p) c -> p k c", p=128), writes=[r_wq[hb]])
                dma("pool", wkv[hb], w_kv[:, h * 256:(h + 1) * 256].rearrange("(k p) c -> p k c", p=128), writes=[r_wkv[hb]])
                for ti, (t0, n) in enumerate(TT):
                    b = 6 + next_bank(0, 2)
                    for k in range(QB):
                        S.op("pe", lambda e, b=b, k=k, t0=t0, n=n, hb=hb: e.matmul(banks[b][:, 0:n], lhsT=wq[hb][:, k, 0:128], rhs=qnT[:, k, t0:t0 + n],
                                                                             start=(k == 0), stop=(k == QB - 1)),
                             reads=[r_wq[hb], r_qn], writes=[rbank[b]])
                    evac_copy(Qn[:, t0:t0 + n], banks[b][:, 0:n], reads=[rbank[b]], writes=[r_Qn])
                    bA = 6 + next_bank(0, 2)
                    for k in range(QB):
                        S.op("pe", lambda e, b=bA, k=k, t0=t0, n=n, hb=hb: e.matmul(banks[b][0:64, 0:n], lhsT=wq[hb][:, k, 128:192], rhs=qnT[:, k, t0:t0 + n],
                                                                              start=(k == 0), stop=(k == QB - 1)),
                             reads=[r_wq[hb], r_qn], writes=[rbank[bA]])
                    q = ti % 2
                    S.op("dve", lambda e, b=bA, t0=t0, n=n, q=q: e.tensor_tensor(out=t1[q][:, 0:n], in0=banks[b][0:64, 0:n], in1=fct[:, t0:t0 + n], op=ALU.mult),
                         reads=[rbank[bA]] + r_pd, writes=[r_t1[q]])
                    bB = 6 + next_bank(0, 2)
                    for k in range(QB):
                        S.op("pe", lambda e, b=bB, k=k, t0=t0, n=n, hb=hb: e.matmul(banks[b][0:64, 0:n], lhsT=wq[hb][:, k, 192:256], rhs=qnT[:, k, t0:t0 + n],
                                                                              start=(k == 0), stop=(k == QB - 1)),
                             reads=[r_wq[hb], r_qn], writes=[rbank[bB]])
                    S.op("dve", lambda e, b=bB, t0=t0, n=n, q=q: e.tensor_tensor(out=t2[q][:, 0:n], in0=banks[b][0:64, 0:n], in1=fst[:, t0:t0 + n], op=ALU.mult),
                         reads=[rbank[bB]] + r_pd, writes=[r_t2[q]])
                    S.op("dve", lambda e, t0=t0, n=n, q=q: e.tensor_tensor(out=Qr[:, t0:t0 + n], in0=t1[q][:, 0:n], in1=t2[q][:, 0:n], op=ALU.add),
                         reads=[r_t1[q], r_t2[q]], writes=[r_Qr])
                for k0 in range(0, NK, 512):
                    n = min(512, NK - k0)
                    b = 6 + next_bank(0, 2)
                    for r in range(RB):
                        S.op("pe", lambda e, b=b, r=r, k0=k0, n=n, hb=hb: e.matmul(banks[b][:, 0:n], lhsT=wkv[hb][:, r, 0:128], rhs=ckvT[:, r, k0:k0 + n],
                                                                             start=(r == 0), stop=(r == RB - 1)),
                             reads=[r_wkv[hb], r_ckv], writes=[rbank[b]])
                    evac_copy(Kh[:, k0:k0 + n], banks[b][:, 0:n], reads=[rbank[b]], writes=[r_Kh])
                for g in range(0, NKB, 4):
                    ng = min(4, NKB - g)
                    b = 6 + next_bank(0, 2)
                    for kb in range(g, g + ng):
                        for r in range(RB):
                            S.op("pe", lambda e, b=b, r=r, kb=kb, g=g, hb=hb: e.matmul(banks[b][:, (kb - g) * 128:(kb - g + 1) * 128],
                                                                                 lhsT=ckvT[:, r, kb * 128:(kb + 1) * 128], rhs=wkv[hb][:, r, 128:256],
                                                                                 start=(r == 0), stop=(r == RB - 1)),
                                 reads=[r_wkv[hb], r_ckv], writes=[rbank[b]])
                    evac_copy(Vh[:, g:g + ng, :], banks[b][:, 0:ng * 128].rearrange("p (k n) -> p k n", n=128), reads=[rbank[b]], writes=[r_Vh])
                oq = h % 2
                units = []
                for qt in range(2):
                    kl = [(kb, 128, ("ctx", kb)) for kb in range(NCTX // 128)]
                    for ob in range(4 * (qt + 1)):
                        kl.append((NCTX // 128 + ob, 128, ("full",) if ob < 4 * qt else ("diag", ob - 4 * qt)))
                    units.append((qt * 512, 512, kl))
                for s in range(2):
                    base = (NCTX + OWN) // 128 + s * (NKS // 128)
                    kl = [(base + i, 128, ("full",)) for i in range(PAST // 128)] + [(base + PAST // 128, NS, ("full",))]
                    units.append((OWN + s * NS, NS, kl))
                def run_unit(q0, nq, kl, oq):
                    u = pstate["u"]; pstate["u"] += 1
                    bO = 2 + (u % 2) * 2; bD = bO + 1
                    pend = None
                    nkl = len(kl)

                    da = dacc[u % 2]; r_da = r_dacc[u % 2]

                    def issue_pv(pp, first, last):
                        (Pap, rP, kp, kb) = pp
                        S.op("pe", lambda e: e.matmul(banks[bO][:, 0:nq], lhsT=Vh[0:kp, kb, :], rhs=Pap[0:kp, 0:nq], start=first, stop=last),
                             reads=[rP, r_Vh], writes=[rbank[bO]])
                        if first:
                            S.op("dve", lambda e: e.tensor_copy(out=da[0:kp, 0:nq], in_=Pap[0:kp, 0:nq]), reads=[rP], writes=[r_da])
                        else:
                            S.op("dve", lambda e: e.tensor_tensor(out=da[0:kp, 0:nq], in0=da[0:kp, 0:nq], in1=Pap[0:kp, 0:nq], op=ALU.add),
                                 reads=[rP], writes=[r_da])
                        if last:
                            S.op("pe", lambda e: e.matmul(banks[bD][:, 0:nq], lhsT=ones_f, rhs=da[:, 0:nq], start=True, stop=True),
                                 reads=[r_da, r_const], writes=[rbank[bD]])

                    for idx, (kb, kp, kind) in enumerate(kl):
                        bS = next_bank(0, 2)
                        S.op("pe", lambda e, bS=bS, kb=kb, kp=kp: e.matmul(banks[bS][0:kp, 0:nq], lhsT=Kh[:, kb * 128:kb * 128 + kp], rhs=Qn[:, q0:q0 + nq],
                                                                         start=True, stop=False),
                             reads=[r_Kh, r_Qn], writes=[rbank[bS]])
                        S.op("pe", lambda e, bS=bS, kb=kb, kp=kp: e.matmul(banks[bS][0:kp, 0:nq], lhsT=krT[:, kb * 128:kb * 128 + kp], rhs=Qr[:, q0:q0 + nq],
                                                                         start=False, stop=True),
                             reads=[r_kr, r_Qr], writes=[rbank[bS]])
                        if kind[0] == "diag":
                            d = kind[1]
                            Pap = Pd[d]; rP = r_Pd[d]
                            if 128 * d + 64 < 512:
                                S.op("act", lambda e, bS=bS, d=d, Pap=Pap: e.activation(out=Pap[:, 128 * d + 64:512], in_=banks[bS][:, 128 * d + 64:512],
                                                                                    func=AF.Exp, scale=SCALE),
                                     reads=[rbank[bS]], writes=[rP])
                            S.op("act", lambda e, bS=bS, d=d, Pap=Pap: e.activation(out=Pap[0:64, 128 * d:128 * d + 64], in_=banks[bS][0:64, 128 * d:128 * d + 64],
                                                                                func=AF.Exp, scale=SCALE),
                                 reads=[rbank[bS]], writes=[rP])
                        else:
                            pi_ = pstate["p"] % 3; pstate["p"] += 1
                            Pap = Pb[pi_]; rP = r_Pb[pi_]
                            if kind[0] == "ctx":
                                S.op("act", lambda e, bS=bS, kp=kp, Pap=Pap, c=kind[1]: e.activation(out=Pap[0:kp, 0:nq], in_=banks[bS][0:kp, 0:nq], func=AF.Exp,
                                                                                                 scale=SCALE, bias=kbt[:, c:c + 1]),
                                     reads=[rbank[bS]] + r_pd, writes=[rP])
                            else:
                                S.op("act", lambda e, bS=bS, kp=kp, Pap=Pap: e.activation(out=Pap[0:kp, 0:nq], in_=banks[bS][0:kp, 0:nq], func=AF.Exp, scale=SCALE),
                                     reads=[rbank[bS]], writes=[rP])
                        if pend is not None:
                            issue_pv(pend[0], pend[1] == 0, False)
                        pend = ((Pap, rP, kp, kb), idx)
                    issue_pv(pend[0], pend[1] == 0, True)
                    rq = u % 2
                    S.op("dve", lambda e, rq=rq: e.reciprocal(out=rden[rq][:, 0:nq], in_=banks[bD][:, 0:nq]), reads=[rbank[bD]], writes=[r_rden[rq]])
                    S.op("dve", lambda e, rq=rq: e.tensor_tensor(out=Ost[oq][:, q0:q0 + nq], in0=banks[bO][:, 0:nq], in1=rden[rq][:, 0:nq], op=ALU.mult),
                         reads=[rbank[bO], r_rden[rq]], writes=[r_Ost[oq]])
                for (q0_, nq_, kl_) in units:
                    run_unit(q0_, nq_, kl_, oq)
                dma("sp", OT[h * 128:(h + 1) * 128, :], Ost[oq], reads=[r_Ost[oq]], writes=[r_scrD])

        if "D" in cfg.get("stages", "ABCDEFG"):
            stage_D()

        def stage_E():
            new_stage()
            groupsE = [[(0, 512)], [(512, 512), (1024, 128)]]
            oT = A16([128, H, 640]); r_oT = R()
            panE = [A16([128, H, 256]) for _ in range(2)]; r_panE = [R(), R()]
            sgE = [A16([128, 512]) for _ in range(2)]; r_sgE = [R(), R()]
            m1E = [A32([128, 512]) for _ in range(2)]; r_m1E = [R(), R()]
            tE = [A32([128, 512]) for _ in range(2)]; r_tE = [R(), R()]
            mgE = [A16([128, 512]) for _ in range(2)]; r_mgE = [R(), R()]
            r_scrE = R()
            ecnt = 0
            for grp in groupsE:
                g0 = grp[0][0]; gn = sum(n for _, n in grp)
                for hh in range(H):
                    dma("sp", oT[:, hh, 0:gn], OT[hh * 128:(hh + 1) * 128, g0:g0 + gn], reads=[], writes=[r_oT])
                for pi in range(D // 256):
                    pb = pi % 2
                    dma("pool", panE[pb], w_o[:, pi * 256:(pi + 1) * 256].rearrange("(k p) c -> p k c", p=128), writes=[r_panE[pb]])
                    for j in range(2):
                        mb = pi * 2 + j
                        for (t0, n) in grp:
                            q = ecnt % 2; ecnt += 1
                            dma("sp", sgE[q][:, 0:n], SGT[(KB + mb) * 128:(KB + mb + 1) * 128, t0:t0 + n], writes=[r_sgE[q]])
                            dma("sp", m1E[q][:, 0:n], M1T[mb * 128:(mb + 1) * 128, t0:t0 + n], writes=[r_m1E[q]])
                            b = next_bank()
                            for k in range(H):
                                S.op("pe", lambda e, b=b, k=k, j=j, t0=t0, n=n, pb=pb, g0=g0: e.matmul(banks[b][:, 0:n], lhsT=panE[pb][:, k, j * 128:(j + 1) * 128],
                                                                                                 rhs=oT[:, k, t0 - g0:t0 - g0 + n], start=(k == 0), stop=(k == H - 1)),
                                     reads=[r_panE[pb], r_oT], writes=[rbank[b]])
                            S.op("dve", lambda e, b=b, n=n, q=q: e.tensor_tensor(out=tE[q][:, 0:n], in0=banks[b][:, 0:n], in1=sgE[q][:, 0:n], op=ALU.mult),
                                 reads=[rbank[b], r_sgE[q]], writes=[r_tE[q]])
                            S.op("dve", lambda e, n=n, q=q: e.tensor_tensor(out=mgE[q][:, 0:n], in0=tE[q][:, 0:n], in1=m1E[q][:, 0:n], op=ALU.add),
                                 reads=[r_tE[q], r_m1E[q]], writes=[r_mgE[q]])
                            dma("sp", MGT[mb * 128:(mb + 1) * 128, t0:t0 + n], mgE[q][:, 0:n], reads=[r_mgE[q]], writes=[r_scrE])

        if "E" in cfg.get("stages", "ABCDEFG"):
            stage_E()

        def stage_F():
            new_stage()
            NTB = T // 128
            mgT = A16([128, KB, T]); r_mg = R()
            for k in range(KB):
                dma("sp", mgT[:, k, :], MGT[k * 128:(k + 1) * 128, :], writes=[r_mg])
            pan = [A16([128, KB, 512]) for _ in range(2)]; r_pan = [R(), R()]
            xtl = [A32([128, 512]) for _ in range(2)]; r_xtl = [R(), R()]
            hpst = [A32([128, 512]) for _ in range(2)]; r_hpst = [R(), R()]
            r_scrF = R()
            fcnt = 0
            for pi in range(D // 512):
                pb = pi % 2
                dma("pool", pan[pb], w_out[:, pi * 512:(pi + 1) * 512].rearrange("(k p) c -> p k c", p=128), writes=[r_pan[pb]])
                for tb in range(NTB):
                    q = fcnt % 2; fcnt += 1
                    dma("sp", xtl[q], x_all[tb * 128:(tb + 1) * 128, pi * 512:(pi + 1) * 512], writes=[r_xtl[q]])
                    b = next_bank()
                    for k in range(KB):
                        S.op("pe", lambda e, b=b, k=k, tb=tb, pb=pb: e.matmul(banks[b], lhsT=mgT[:, k, tb * 128:(tb + 1) * 128], rhs=pan[pb][:, k, :],
                                                                        start=(k == 0), stop=(k == KB - 1)),
                             reads=[r_pan[pb], r_mg], writes=[rbank[b]])
                    S.op("dve", lambda e, b=b, q=q: e.scalar_tensor_tensor(out=hpst[q], in0=xtl[q], scalar=ALPHA, in1=banks[b], op0=ALU.mult, op1=ALU.add),
                         reads=[rbank[b], r_xtl[q]], writes=[r_hpst[q]])
                    dma("sp", HP[tb * 128:(tb + 1) * 128, pi * 512:(pi + 1) * 512], hpst[q], reads=[r_hpst[q]], writes=[r_scrF])
            new_stage()

            g1t = A32([128, D]); b1t = A32([128, D])
            r_g1 = [load_small(g1t, ln1g[0:1, :].partition_broadcast(128)), load_small(b1t, ln1b[0:1, :].partition_broadcast(128))]
            hpt = [A32([128, D]) for _ in range(2)]; r_hpt = [R(), R()]
            junkb = A16([128, D]); r_junkb = R()
            hbf = [A16([128, D]) for _ in range(2)]; r_hbf = [R(), R()]
            hTst = [A16([128, KB, 128]) for _ in range(2)]; r_hTst = [R(), R()]
            stF = [A32([128, 8]) for _ in range(2)]; r_stF = [R(), R()]
            for tb in range(NTB):
                q = tb % 2
                dma("sp", hpt[q], HP[tb * 128:(tb + 1) * 128, :], writes=[r_hpt[q]])
                layer_norm_rows(hpt[q], r_hpt[q], g1t, b1t, r_g1, junkb, r_junkb, stF[q], r_stF[q])
                dma("sp", HS[tb * 128:(tb + 1) * 128, :], hpt[q], reads=[r_hpt[q]], writes=[r_scrF])
                S.op("act", lambda e, q=q: e.activation(out=hbf[q], in_=hpt[q], func=AF.Copy), reads=[r_hpt[q]], writes=[r_hbf[q]])
                for g in range(0, KB, 8):
                    b = next_bank(); ng = min(8, KB - g)
                    for k in range(g, g + ng):
                        S.op("pe", lambda e, b=b, k=k, g=g, q=q: e.transpose(banks16[b][:, (k - g) * 128:(k - g + 1) * 128], hbf[q][:, k * 128:(k + 1) * 128], ident_b),
                             reads=[r_hbf[q], r_const], writes=[rbank[b]])
                    evac_copy(hTst[q][:, g:g + ng, :], banks16[b][:, 0:ng * 128].rearrange("p (k n) -> p k n", n=128), reads=[rbank[b]], writes=[r_hTst[q]])
                dma("sp", HT[:, tb * 128:(tb + 1) * 128].rearrange("(k p) n -> p k n", p=128), hTst[q], reads=[r_hTst[q]], writes=[r_scrF])

        if "F" in cfg.get("stages", "ABCDEFG"):
            stage_F()

        def stage_G():
            groupsG = [([0, 1, 2, 3, 8], [(0, 512), (1024, 128)]), ([4, 5, 6, 7], [(512, 512)])]
            NCH = DFF // 512
            def ffn_group(tbs, tiles):
                new_stage()
                gn = sum(n for _, n in tiles)
                hTg = A16([128, KB, gn]); r_hTg = R()
                c0 = 0
                loc = []
                for (t0, n) in tiles:
                    for k in range(KB):
                        dma("sp", hTg[:, k, c0:c0 + n], HT[k * 128:(k + 1) * 128, t0:t0 + n], writes=[r_hTg])
                    loc.append((c0, n)); c0 += n
                facc = A32([128, len(tbs), D]); r_facc = [[R() for _ in range(D // 512)] for _ in tbs]
                for i, tb in enumerate(tbs):
                    dma("sp", facc[:, i, :], HS[tb * 128:(tb + 1) * 128, :], writes=r_facc[i])
                    S.op("act", lambda e, i=i: e.activation(out=facc[:, i, :], in_=facc[:, i, :], func=AF.Identity, scale=ALPHA), reads=[], writes=r_facc[i])
                MARK = state["top"]
                wup = [A16([128, KB, 128]) for _ in range(3)]; r_wup = [R() for _ in range(3)]
                wdn = [A16([128, 4, 2048]) for _ in range(2)]; r_wdn = [R(), R()]
                aT = [A16([128, 4, gn]) for _ in range(2)]; r_aT = [R(), R()]
                rt = [A32([128, 512]) for _ in range(2)]; r_rt2 = [R(), R()]
                ucnt = 0; dcnt = 0; rcnt = 0
                nhalf = D // 2048 if D >= 2048 else 1
                hw = D // nhalf
                cnts = {"u": 0, "d": 0, "r": 0}
                def ffn_up(c):
                    ab = c % 2
                    for hbk in range(4):
                        hid = 4 * c + hbk
                        wb = cnts["u"] % 3; cnts["u"] += 1
                        dma("pool", wup[wb], w_up[:, hid * 128:(hid + 1) * 128].rearrange("(k p) c -> p k c", p=128), writes=[r_wup[wb]])
                        for (l0, n) in loc:
                            b = next_bank(0, 4)
                            for k in range(KB):
                                S.op("pe", lambda e, b=b, k=k, wb=wb, l0=l0, n=n: e.matmul(banks[b][:, 0:n], lhsT=wup[wb][:, k, :], rhs=hTg[:, k, l0:l0 + n],
                                                                                     start=(k == 0), stop=(k == KB - 1)),
                                     reads=[r_wup[wb], r_hTg], writes=[rbank[b]])
                            q = cnts["r"] % 2; cnts["r"] += 1
                            S.op("act", lambda e, b=b, n=n, q=q: e.activation(out=rt[q][:, 0:n], in_=banks[b][:, 0:n], func=AF.Relu), reads=[rbank[b]], writes=[r_rt2[q]])
                            S.op("dve", lambda e, n=n, q=q, ab=ab, hbk=hbk, l0=l0: e.tensor_tensor(out=aT[ab][:, hbk, l0:l0 + n], in0=rt[q][:, 0:n], in1=rt[q][:, 0:n], op=ALU.mult),
                                 reads=[r_rt2[q]], writes=[r_aT[ab]])
                def ffn_down(c):
                    ab = c % 2
                    for hf in range(nhalf):
                        db = cnts["d"] % 2; cnts["d"] += 1
                        dma("pool", wdn[db][:, :, 0:hw], w_down[c * 512:(c + 1) * 512, hf * hw:(hf + 1) * hw].rearrange("(k p) n -> p k n", p=128), writes=[r_wdn[db]])
                        for i in range(len(tbs)):
                            for nt in range(hw // 512):
                                b = 4 + next_bank(0, 4)
                                for hbk in range(4):
                                    S.op("pe", lambda e, b=b, hbk=hbk, i=i, nt=nt, db=db, ab=ab: e.matmul(banks[b], lhsT=aT[ab][:, hbk, i * 128:(i + 1) * 128],
                                                                                                    rhs=wdn[db][:, hbk, nt * 512:(nt + 1) * 512],
                                                                                                    start=(hbk == 0), stop=(hbk == 3)),
                                         reads=[r_wdn[db], r_aT[ab]], writes=[rbank[b]])
                                col = hf * hw + nt * 512
                                S.op("dve", lambda e, b=b, i=i, col=col: e.tensor_tensor(out=facc[:, i, col:col + 512], in0=banks[b], in1=facc[:, i, col:col + 512], op=ALU.add),
                                     reads=[rbank[b]], writes=[r_facc[i][col // 512]])
                ffn_up(0)
                for c_ in range(NCH):
                    if c_ + 1 < NCH:
                        ffn_up(c_ + 1)
                    ffn_down(c_)
                S.barrier()
                state["top"] = MARK
                g2t = A32([128, D]); b2t = A32([128, D])
                r_g2 = [load_small(g2t, ln2g[0:1, :].partition_broadcast(128)), load_small(b2t, ln2b[0:1, :].partition_broadcast(128))]
                junk2 = A16([128, D]); r_junk2 = R()
                stG = [A32([128, 8]) for _ in range(2)]; r_stG = [R(), R()]
                for i, tb in enumerate(tbs):
                    rr = R()
                    layer_norm_rows(facc[:, i, :], rr, g2t, b2t, r_g2, junk2, r_junk2, stG[i % 2], r_stG[i % 2])
                    dma("sp", y_out[tb * 128:(tb + 1) * 128, :], facc[:, i, :], reads=[rr])

            for (tbs_, tiles_) in groupsG:
                ffn_group(tbs_, tiles_)
        if "G" in cfg.get("stages", "ABCDEFG"):
            stage_G()

        S.emit()
    return nc


_CACHE = {}


def _prep_inputs(inp):
    f = lambda a: np.ascontiguousarray(np.asarray(a, dtype=np.float32))
    x_prompt = f(inp["x_prompt"]); x_sample = f(inp["x_sample"])
    B, SEQ, D = x_prompt.shape
    assert SEQ == 4 * OWN and x_sample.shape[1] == NS and x_sample.shape[0] == 16 and B == 2
    cache_ckv = f(inp["cache_ckv"])[0]; cache_kr = f(inp["cache_krope"])[0]; state_conv = f(inp["state_conv"])[0]
    KVR = cache_ckv.shape[2]
    assert cache_ckv.shape[1] == PAST
    w_in = f(inp["w_in"])[0]; b_in = f(inp["b_in"])[0]
    QR = inp["q_a_g"].shape[1]
    w_q_b = f(inp["w_q_b"])[0]
    H = w_q_b.shape[1] // 192
    DFF = inp["w_up"].shape[2]
    CD = D
    cfg = dict(D=D, QR=QR, KVR=KVR, H=H, DFF=DFF, ALPHA=float(2.0 ** 0.25))
    KB = D // 128
    o = 0
    wga = w_in[:, o:o + CD]; bga = b_in[o:o + CD]; o += CD
    wgb = w_in[:, o:o + CD]; bgb = b_in[o:o + CD]; o += CD
    wql = w_in[:, o:o + QR]; bql = b_in[o:o + QR]; o += QR
    wkl = w_in[:, o:o + KVR + ROPE]; bkl = b_in[o:o + KVR + ROPE]; o += KVR + ROPE
    wgate = w_in[:, o:o + 2 * D]; bgate = b_in[o:o + 2 * D]
    w_glu = np.concatenate([wga.reshape(D, CD // 256, 256), wgb.reshape(D, CD // 256, 256)], axis=2).reshape(D, 2 * CD)
    pm = lambda v: np.ascontiguousarray(v.reshape(-1, 128).T)
    wq3 = w_q_b.reshape(QR, H, 192)
    w_q = np.concatenate([wq3[:, :, 0:128], wq3[:, :, 128:192], wq3[:, :, 160:192], wq3[:, :, 128:160]], axis=2).reshape(QR, H * 256)
    half = ROPE // 2
    inv = (10000.0 ** (-np.arange(half, dtype=np.float32) / half)).astype(np.float32)

    def tables(pos):
        ang = pos.astype(np.float32)[:, None] * inv[None, :]
        return np.cos(ang).astype(np.float32), np.sin(ang).astype(np.float32)

    shared = dict(
        w_glu=np.ascontiguousarray(w_glu), w_ql=np.ascontiguousarray(wql), w_kl=np.ascontiguousarray(wkl),
        w_gate=np.ascontiguousarray(wgate),
        b_ga=pm(bga), b_gb=pm(bgb), b_ql=pm(bql), b_gate=pm(bgate), b_kl=np.ascontiguousarray(bkl[None, :]),
        wdw=np.ascontiguousarray(f(inp["w_dw"])[0].T.reshape(CD // 128, 128, CW).transpose(1, 0, 2).reshape(128, -1)),
        bdw=pm(f(inp["b_dw"])[0]), cng=pm(f(inp["conv_ln_g"])[0]), cnb=pm(f(inp["conv_ln_b"])[0]),
        w_pw=f(inp["w_conv_pw"])[0], qag=pm(f(inp["q_a_g"])[0]), w_q=np.ascontiguousarray(w_q),
        kvg=f(inp["kv_a_g"]), w_kv=f(inp["w_kv_b"])[0], w_o=f(inp["w_attn_o"])[0], w_out=f(inp["w_out"])[0],
        ln1g=f(inp["ln1_g"]), ln1b=f(inp["ln1_b"]), w_up=f(inp["w_up"])[0], w_down=f(inp["w_down"])[0],
        ln2g=f(inp["ln2_g"]), ln2b=f(inp["ln2_b"]),
    )
    in_maps = []
    for c in range(8):
        b = c // 4; j = c % 4; s0 = j * OWN
        m = dict(shared)
        m["x_all"] = np.concatenate([x_prompt[b, s0:s0 + OWN], x_sample[2 * c], x_sample[2 * c + 1]], axis=0)
        m["x_halo"] = x_prompt[b, s0 - 32:s0].copy() if j > 0 else np.zeros((32, D), np.float32)
        m["x_ctx"] = x_prompt[b, 0:NCTX]
        m["cache_ckv"] = np.concatenate([cache_ckv[2 * c], cache_ckv[2 * c + 1]], axis=0)
        m["cache_kr"] = np.concatenate([cache_kr[2 * c], cache_kr[2 * c + 1]], axis=0)
        sc = np.zeros((64, CD), np.float32)
        sc[0:30] = state_conv[2 * c]; sc[32:62] = state_conv[2 * c + 1]
        m["state_conv"] = sc
        m["halo_valid"] = np.full((128, 1), 1.0 if j > 0 else 0.0, np.float32)
        kbv = np.where(np.arange(NCTX) < s0, 0.0, -30000.0).astype(np.float32)
        m["keybias"] = np.ascontiguousarray(kbv.reshape(-1, 128).T)
        pos = np.concatenate([s0 + np.arange(OWN), PAST + np.arange(NS), PAST + np.arange(NS), np.arange(NCTX)])
        co, si = tables(pos)
        m["tm_cos"] = co; m["tm_sin"] = si
        m["fm_c"] = np.ascontiguousarray(np.concatenate([co[:T], co[:T]], axis=1).T)
        m["fm_s"] = np.ascontiguousarray(np.concatenate([-si[:T], si[:T]], axis=1).T)
        in_maps.append(m)
    return cfg, in_maps


def _assemble(cfg, res, B=2, SEQ=4 * OWN):
    D = cfg["D"]; KVR = cfg["KVR"]
    y_p = np.zeros((B, SEQ, D), np.float32); y_s = np.zeros((16, NS, D), np.float32)
    ckv_p = np.zeros((1, B, SEQ, KVR), np.float32); kr_p = np.zeros((1, B, SEQ, ROPE), np.float32)
    cs_p = np.zeros((1, B, CW - 1, D), np.float32)
    ckv_s = np.zeros((1, 16, NS, KVR), np.float32); kr_s = np.zeros((1, 16, NS, ROPE), np.float32)
    cs_s = np.zeros((1, 16, CW - 1, D), np.float32)
    for c in range(8):
        r = res[c]; b = c // 4; j = c % 4; s0 = j * OWN
        y_p[b, s0:s0 + OWN] = r["y_out"][0:OWN]
        ckv_p[0, b, s0:s0 + OWN] = r["ckv_out"][0:OWN]; kr_p[0, b, s0:s0 + OWN] = r["kr_out"][0:OWN]
        if j == 3:
            cs_p[0, b] = r["cs_out"][0:30]
        for s in range(2):
            y_s[2 * c + s] = r["y_out"][OWN + s * NS:OWN + (s + 1) * NS]
            ckv_s[0, 2 * c + s] = r["ckv_out"][OWN + s * NS:OWN + (s + 1) * NS]
            kr_s[0, 2 * c + s] = r["kr_out"][OWN + s * NS:OWN + (s + 1) * NS]
            cs_s[0, 2 * c + s] = r["cs_out"][32 + 32 * s:62 + 32 * s]
    return (y_p, y_s, ckv_p, kr_p, cs_p, ckv_s, kr_s, cs_s)


def kernel(_stages=None, _debug=False, _extra=None, **inputs):
    cfg, in_maps = _prep_inputs(inputs)
    cfg.update(_extra or {})
    if "maxops" in cfg:
        Sched.maxops = cfg["maxops"]; Sched.nrec = 0; Sched.log = []
    if _stages is not None:
        cfg["stages"] = _stages
        nc = build(cfg, debug=_debug)
        ncore = cfg.get("ncore", 8)
        res = run_bass_kernel_spmd(nc, in_maps[:ncore], core_ids=list(range(ncore)))
        return cfg, in_maps, res.results
    key = tuple(sorted(cfg.items()))
    if key not in _CACHE:
        _CACHE[key] = build(cfg)
    nc = _CACHE[key]
    res = run_bass_kernel_spmd(nc, in_maps, core_ids=list(range(8)))
    return _assemble(cfg, res.results)
```

```python
import contextlib
import numpy as np
import concourse.bass as bass
import concourse.mybir as mybir
from concourse.bass_utils import run_bass_kernel_spmd

F32 = mybir.dt.float32
BF16 = mybir.dt.bfloat16
ALU = mybir.AluOpType
AF = mybir.ActivationFunctionType

OWN = 1024
NS = 64
T = OWN + 2 * NS
NCTX = 3072
PAST = 1024
NKS = PAST + NS + 64
NK = NCTX + OWN + 2 * NKS
TT = [(0, 512), (512, 512), (1024, 128)]
ROPE = 64
CW = 31
LN_EPS = 1e-5
RMS_EPS = 1e-6
ARENA_WORDS = 52992


class R:
    __slots__ = ("w", "rd", "ps")

    def __init__(self):
        self.w = None
        self.rd = {}
        self.ps = None


class Sched:
    ENGS = ("pe", "act", "dve", "pool", "sp")
    NDMA = 24
    NPOOL = 16

    def __init__(self, nc):
        self.nc = nc
        self.streams = {e: [] for e in self.ENGS}
        self.count = {e: 0 for e in self.ENGS}
        self.seen = {e: {} for e in self.ENGS}
        self.dcount = {("d", i): 0 for i in range(self.NDMA)}
        self.dnext = 0
        self.pcount = {("p", i): 0 for i in range(self.NPOOL)}
        self.pnext = 0

    def recycle(self):
        pass

    def _wait(self, eng, key, val):
        if self.seen[eng].get(key, 0) >= val:
            return
        self.seen[eng][key] = val
        self.streams[eng].append(("w", key, val))

    maxops = None
    nrec = 0
    log = []

    def op(self, eng, fn, reads=(), writes=(), dma=False):
        Sched.nrec += 1
        if Sched.maxops is not None:
            import sys as _sys
            Sched.log.append((Sched.nrec, eng, _sys._getframe(1).f_lineno, _sys._getframe(2).f_lineno))
            if Sched.nrec > Sched.maxops:
                return None
        deps = {}
        for r in reads:
            if r.w is not None:
                k, v, s = r.w
                if deps.get(k, (0, None))[0] < v:
                    deps[k] = (v, s)
        for r in writes:
            if r.w is not None:
                k, v, s = r.w
                if deps.get(k, (0, None))[0] < v:
                    deps[k] = (v, s)
            for k, (v, s) in r.rd.items():
                if deps.get(k, (0, None))[0] < v:
                    deps[k] = (v, s)
        for k, (v, s) in deps.items():
            if s == "pe" and eng == "pe" and not dma:
                continue
            self._wait(eng, k, v)
        if dma and eng == "pool":
            key = ("p", self.pnext)
            self.pnext = (self.pnext + 1) % self.NPOOL
            if self.pcount[key] > 0:
                self._wait(eng, key, self.pcount[key])
            self.pcount[key] += 16
            tok = (key, self.pcount[key], "dma")
            self.streams[eng].append(("o", fn, key, 16))
        elif dma:
            key = ("d", self.dnext)
            self.dnext = (self.dnext + 1) % self.NDMA
            if self.dcount[key] > 0:
                self._wait(eng, key, self.dcount[key])
            self.dcount[key] += 16
            tok = (key, self.dcount[key], "dma")
            self.streams[eng].append(("o", fn, key, 16))
        else:
            self.count[eng] += 1
            tok = (eng, self.count[eng], eng)
            self.streams[eng].append(("o", fn, eng, 1))
        k, v, s = tok
        for r in reads:
            if r.rd.get(k, (0, None))[0] < v:
                r.rd[k] = (v, s)
        for r in writes:
            r.w = tok
            r.rd = {}
        return tok

    def barrier(self):
        for e in self.ENGS:
            for o in self.ENGS:
                if self.count[o] > 0:
                    self._wait(e, o, self.count[o])
            for k, v in self.dcount.items():
                if v > 0:
                    self._wait(e, k, v)
            for k, v in self.pcount.items():
                if v > 0:
                    self._wait(e, k, v)

    def emit(self):
        nc = self.nc
        with contextlib.ExitStack() as es:
            sems = {}
            for e in self.ENGS:
                sems[e] = es.enter_context(nc.semaphore("s_" + e))
            for k in self.dcount:
                sems[k] = es.enter_context(nc.semaphore("s_d%d" % k[1]))
            for k in self.pcount:
                sems[k] = es.enter_context(nc.semaphore("s_p%d" % k[1]))
            self.barrier()
            block = es.enter_context(nc.Block())

            def run(engname):
                def body(eng):
                    for it in self.streams[engname]:
                        if it[0] == "w":
                            eng.wait_ge(sems[it[1]], it[2])
                        elif it[0] == "c":
                            eng.sem_clear(sems[it[1]])
                        else:
                            it[1](eng).then_inc(sems[it[2]], it[3])
                return body

            block.tensor(run("pe"))
            block.scalar(run("act"))
            block.vector(run("dve"))
            block.gpsimd(run("pool"))
            block.sync(run("sp"))


def build(cfg, debug=False):
    D = cfg["D"]; QR = cfg["QR"]; KVR = cfg["KVR"]; H = cfg["H"]; DFF = cfg["DFF"]
    CD = D
    KB = D // 128; CB = CD // 128; QB = QR // 128; RB = KVR // 128
    KLW = KVR + ROPE
    ALPHA = cfg["ALPHA"]
    SCALE = 192.0 ** -0.5
    nc = bass.Bass("TRN2", target_bir_lowering=False)

    def din(name, shape, dt=F32):
        return nc.dram_tensor(name, list(shape), dt, kind="ExternalInput").ap()

    def dout(name, shape, dt=F32):
        return nc.dram_tensor(name, list(shape), dt, kind="ExternalOutput").ap()

    def dscr(name, shape, dt):
        kind = "ExternalOutput" if debug else "Internal"
        return nc.dram_tensor(name, list(shape), dt, kind=kind).ap()

    x_all = din("x_all", [T, D]); x_halo = din("x_halo", [32, D]); x_ctx = din("x_ctx", [NCTX, D])
    cache_ckv = din("cache_ckv", [2 * PAST, KVR]); cache_kr = din("cache_kr", [2 * PAST, ROPE])
    state_conv = din("state_conv", [64, CD])
    w_glu = din("w_glu", [D, 2 * CD]); w_ql = din("w_ql", [D, QR]); w_kl = din("w_kl", [D, KLW])
    w_gate = din("w_gate", [D, 2 * D])
    b_ga = din("b_ga", [128, CB]); b_gb = din("b_gb", [128, CB]); b_ql = din("b_ql", [128, QB])
    b_gate = din("b_gate", [128, 2 * KB]); b_kl = din("b_kl", [1, KLW])
    wdw = din("wdw", [128, CB * CW]); bdw = din("bdw", [128, CB])
    cng = din("cng", [128, CB]); cnb = din("cnb", [128, CB])
    w_pw = din("w_pw", [CD, D])
    qag = din("qag", [128, QB]); w_q = din("w_q", [QR, H * 256])
    kvg = din("kvg", [1, KVR]); w_kv = din("w_kv", [KVR, H * 256])
    w_o = din("w_o", [H * 128, D]); w_out = din("w_out", [D, D])
    ln1g = din("ln1g", [1, D]); ln1b = din("ln1b", [1, D])
    w_up = din("w_up", [D, DFF]); w_down = din("w_down", [DFF, D])
    ln2g = din("ln2g", [1, D]); ln2b = din("ln2b", [1, D])
    halo_valid = din("halo_valid", [128, 1]); keybias = din("keybias", [128, NCTX // 128])
    tm_cos = din("tm_cos", [T + NCTX, 32]); tm_sin = din("tm_sin", [T + NCTX, 32])
    fm_c = din("fm_c", [64, T]); fm_s = din("fm_s", [64, T])

    y_out = dout("y_out", [T, D]); ckv_out = dout("ckv_out", [T, KVR]); kr_out = dout("kr_out", [T, ROPE])
    cs_out = dout("cs_out", [96, CD])

    YT = dscr("YT", [CD, T], BF16); SGT = dscr("SGT", [2 * D, T], BF16); M1T = dscr("M1T", [D, T], F32)
    QNT = dscr("QNT", [QR, T], BF16); CKVT = dscr("CKVT", [KVR, NK], BF16); KRT = dscr("KRT", [64, NK], BF16)
    OT = dscr("OT", [H * 128, T], BF16); MGT = dscr("MGT", [D, T], BF16)
    HP = dscr("HP", [T, D], F32); HS = dscr("HS", [T, D], F32); HT = dscr("HT", [D, T], BF16)

    with contextlib.ExitStack() as es:
        arena = es.enter_context(nc.sbuf_tensor("arena", [128, ARENA_WORDS], F32))
        psum = es.enter_context(nc.psum_tensor("psum", [128, 4096], F32))
        S = Sched(nc)
        state = {"top": 0}

        def alloc_f32(n):
            n = (n + 7) // 8 * 8
            o = state["top"]; state["top"] += n
            assert state["top"] <= ARENA_WORDS, ("arena overflow", state["top"])
            return arena[:, o:o + n]

        def A32(shape):
            n = int(np.prod(shape[1:]))
            ap = alloc_f32(n)[0:shape[0], 0:n]
            if len(shape) == 3:
                ap = ap.rearrange("p (a b) -> p a b", b=shape[2])
            return ap

        def A16(shape):
            n = int(np.prod(shape[1:]))
            assert n % 2 == 0
            ap = alloc_f32(n // 2).bitcast(BF16)[0:shape[0], 0:n]
            if len(shape) == 3:
                ap = ap.rearrange("p (a b) -> p a b", b=shape[2])
            return ap

        banks = [psum[:, b * 512:(b + 1) * 512] for b in range(8)]
        banks16 = [psum[:, b * 512:(b + 1) * 512].bitcast(BF16) for b in range(8)]
        rbank = [R() for _ in range(8)]
        bstate = {"i": 0}

        def next_bank(lo=0, hi=8):
            i = bstate["i"]
            b = lo + i % (hi - lo)
            bstate["i"] = i + 1
            return b

        def dma(eng, out, in_, reads=(), writes=()):
            return S.op(eng, lambda e: e.dma_start(out=out, in_=in_), reads=reads, writes=writes, dma=True)

        cp_state = {"i": 0}

        def evac_copy(out, in_, reads, writes, same=False):
            if not same:
                cp_state["i"] += 1
            mode = cfg.get("evac", "alt")
            if (mode == "alt" and cp_state["i"] % 2) or mode == "act":
                return S.op("act", lambda e: e.activation(out=out, in_=in_, func=AF.Copy), reads=reads, writes=writes)
            return S.op("dve", lambda e: e.tensor_copy(out=out, in_=in_), reads=reads, writes=writes)

        ident_f = A32([128, 128]); ident_b = A16([128, 128]); ones_b = A16([128, 128]); ones_f = A32([128, 128])
        r_const = R()
        S.op("pool", lambda e: e.memset(ident_f, 0.0), writes=[r_const])
        S.op("pool", lambda e: e.affine_select(out=ident_f, in_=ident_f, pattern=[[-1, 128]],
                                               compare_op=ALU.not_equal, fill=1.0, base=0,
                                               channel_multiplier=1), reads=[r_const], writes=[r_const])
        S.op("dve", lambda e: e.tensor_copy(out=ident_b, in_=ident_f), reads=[r_const], writes=[r_const])
        S.op("dve", lambda e: e.memset(ones_b, 1.0), writes=[r_const])
        S.op("dve", lambda e: e.memset(ones_f, 1.0), writes=[r_const])
        PERM_TOP = state["top"]

        def new_stage():
            S.barrier()
            S.recycle()
            state["top"] = PERM_TOP
            for r in rbank:
                r.w = None; r.rd = {}

        def load_small(dst, src):
            r = R()
            dma("sp", dst, src, writes=[r])
            return r

        xT = A16([128, KB, T]); r_xT = [R() for _ in range(T // 128)]
        xTh = A16([128, KB, 32]); r_xTh = R()

        def layer_norm_rows(hp_t, r_hp, gbt, bbt, r_gb, junk_t, r_junk_, stt, r_stt):
            S.op("dve", lambda e: e.memset(stt[:, 0:2], 0.0), writes=[r_stt])
            S.op("act", lambda e: e.activation(out=junk_t, in_=hp_t, func=AF.Identity, accum_out=stt[:, 0:1]), reads=[r_hp], writes=[r_junk_, r_stt])
            S.op("act", lambda e: e.activation(out=junk_t, in_=hp_t, func=AF.Square, accum_out=stt[:, 1:2]), reads=[r_hp], writes=[r_junk_, r_stt])
            S.op("dve", lambda e: e.tensor_scalar(out=stt[:, 2:3], in0=stt[:, 0:1], scalar1=1.0 / D, scalar2=None, op0=ALU.mult), reads=[r_stt], writes=[r_stt])
            S.op("dve", lambda e: e.tensor_tensor(out=stt[:, 3:4], in0=stt[:, 2:3], in1=stt[:, 2:3], op=ALU.mult), reads=[r_stt], writes=[r_stt])
            S.op("dve", lambda e: e.scalar_tensor_tensor(out=stt[:, 4:5], in0=stt[:, 1:2], scalar=1.0 / D, in1=stt[:, 3:4], op0=ALU.mult, op1=ALU.subtract),
                 reads=[r_stt], writes=[r_stt])
            S.op("act", lambda e: e.activation(out=stt[:, 5:6], in_=stt[:, 4:5], func=AF.Sqrt, bias=LN_EPS), reads=[r_stt], writes=[r_stt])
            S.op("dve", lambda e: e.reciprocal(out=stt[:, 6:7], in_=stt[:, 5:6]), reads=[r_stt], writes=[r_stt])
            S.op("dve", lambda e: e.tensor_scalar(out=hp_t, in0=hp_t, scalar1=stt[:, 2:3], scalar2=stt[:, 6:7], op0=ALU.subtract, op1=ALU.mult),
                 reads=[r_stt], writes=[r_hp])
            S.op("dve", lambda e: e.tensor_tensor(out=hp_t, in0=hp_t, in1=gbt, op=ALU.mult), reads=r_gb, writes=[r_hp])
            S.op("dve", lambda e: e.tensor_tensor(out=hp_t, in0=hp_t, in1=bbt, op=ALU.add), reads=r_gb, writes=[r_hp])


        def stage_A():
            wkl = A16([128, KB, KLW]); r_wkl = R()
            bklb = A32([128, KLW]); kvgb = A32([128, KVR])
            r_bkl = load_small(bklb, b_kl[0:1, :].partition_broadcast(128))
            r_kvg = load_small(kvgb, kvg[0:1, :].partition_broadcast(128))
            dma("pool", wkl, w_kl.rearrange("(k p) c -> p k c", p=128), writes=[r_wkl])
            xrow = [A16([128, D]) for _ in range(2)]; r_xrow = [R(), R()]
            xTb = [A16([128, KB, 128]) for _ in range(2)]; r_xTb = [R(), R()]
            klf = [A32([128, KLW]) for _ in range(2)]; r_klf = [R(), R()]
            ckf = [A32([128, KLW]) for _ in range(2)]; r_ckf = [R(), R()]
            cbf = [A16([128, KLW]) for _ in range(2)]; r_cbf = [R(), R()]
            junk = A32([128, KVR]); r_junk = R()
            cs_t = [A32([128, 64]) for _ in range(2)]; r_cs = [R(), R()]
            st_t = [A32([128, 8]) for _ in range(2)]; r_st = [R(), R()]
            rt_t = [A32([128, 128]) for _ in range(2)]; r_rt = [R(), R()]
            ckst = [A16([128, RB, 128]) for _ in range(2)]; r_ckst = [R(), R()]
            krst = [A16([64, 128]) for _ in range(2)]; r_krst = [R(), R()]
            r_scrA = R()

            hrow = A16([32, D]); r_hrow = R()
            dma("pool", hrow, x_halo[:, :], writes=[r_hrow])
            for g in range(0, KB, 8):
                b = next_bank(); ng = min(8, KB - g)
                for k in range(g, g + ng):
                    S.op("pe", lambda e, b=b, k=k, g=g: e.transpose(banks16[b][:, (k - g) * 32:(k - g) * 32 + 32],
                                                                   hrow[:, k * 128:(k + 1) * 128], ident_b[0:32, 0:32]),
                         reads=[r_hrow, r_const], writes=[rbank[b]])
                evac_copy(xTh[:, g:g + ng, :], banks16[b][:, 0:ng * 32].rearrange("p (k n) -> p k n", n=32),
                          reads=[rbank[b]], writes=[r_xTh])

            def latent_tail(i, key_cols, out_rows):
                b = next_bank()
                for r in range(RB):
                    S.op("pe", lambda e, b=b, r=r: e.transpose(banks16[b][:, r * 128:(r + 1) * 128],
                                                               cbf[i][:, r * 128:(r + 1) * 128], ident_b),
                         reads=[r_cbf[i], r_const], writes=[rbank[b]])
                S.op("pe", lambda e, b=b: e.transpose(banks16[b][0:64, RB * 128:(RB + 1) * 128],
                                                      cbf[i][:, KVR:KLW], ident_b),
                     reads=[r_cbf[i], r_const], writes=[rbank[b]])
                evac_copy(ckst[i], banks16[b][:, 0:RB * 128].rearrange("p (k n) -> p k n", n=128),
                          reads=[rbank[b]], writes=[r_ckst[i]])
                evac_copy(krst[i], banks16[b][0:64, RB * 128:(RB + 1) * 128], reads=[rbank[b]], writes=[r_krst[i]], same=True)
                for (c0, n, k0) in (key_cols if "s" not in cfg.get("skip", "") else []):
                    dma("sp", CKVT[:, k0:k0 + n].rearrange("(r p) n -> p r n", p=128), ckst[i][:, :, c0:c0 + n],
                        reads=[r_ckst[i]], writes=[r_scrA])
                    dma("sp", KRT[:, k0:k0 + n], krst[i][:, c0:c0 + n], reads=[r_krst[i]], writes=[r_scrA])

            zpad = A16([128, RB, 64]); r_zpad = R()
            S.op("dve", lambda e: e.memset(zpad, 0.0), writes=[r_zpad])
            for s_ in range(2):
                kz = NCTX + OWN + s_ * NKS + PAST + NS
                dma("sp", CKVT[:, kz:kz + 64].rearrange("(r p) n -> p r n", p=128), zpad, reads=[r_zpad], writes=[r_scrA])
                dma("sp", KRT[:, kz:kz + 64], zpad[0:64, 0, :], reads=[r_zpad], writes=[r_scrA])
            blocks = []
            for i in range(OWN // 128):
                blocks.append(("x", x_all[i * 128:(i + 1) * 128, :], i, i * 128, [(0, 128, NCTX + i * 128)], i * 128))
            blocks.append(("x", x_all[OWN:T, :], OWN // 128, OWN, [(0, 64, NCTX + OWN + PAST), (64, 64, NCTX + OWN + NKS + PAST)], OWN))
            for i in range(NCTX // 128):
                blocks.append(("c", x_ctx[i * 128:(i + 1) * 128, :], None, T + i * 128, [(0, 128, i * 128)], None))
            for s in range(2):
                for i in range(PAST // 128):
                    blocks.append(("k", s * PAST + i * 128, None, None, [(0, 128, NCTX + OWN + s * NKS + i * 128)], None))

            for bi, blk in list(enumerate(blocks))[cfg.get("blk0", 0):cfg.get("nblk", 10 ** 6)]:
                i = bi % 2
                kind = blk[0]
                if cfg.get("blkbar"):
                    S.barrier()
                if kind == "k":
                    r0 = blk[1]
                    dma("pool", cbf[i][:, 0:KVR], cache_ckv[r0:r0 + 128, :], writes=[r_cbf[i]])
                    dma("pool", cbf[i][:, KVR:KLW], cache_kr[r0:r0 + 128, :], writes=[r_cbf[i]])
                    latent_tail(i, blk[4], None)
                    continue
                _, src, tb, trow, key_cols, orow = blk
                dma("pool", xrow[i], src, writes=[r_xrow[i]])
                if kind == "x":
                    dstT = xT[:, :, tb * 128:(tb + 1) * 128]; r_dst = r_xT[tb]
                else:
                    dstT = xTb[i]; r_dst = r_xTb[i]
                for g in range(0, KB, 8):
                    b = next_bank(); ng = min(8, KB - g)
                    for k in range(g, g + ng):
                        S.op("pe", lambda e, b=b, k=k, g=g, i=i: e.transpose(banks16[b][:, (k - g) * 128:(k - g + 1) * 128],
                                                                            xrow[i][:, k * 128:(k + 1) * 128], ident_b),
                             reads=[r_xrow[i], r_const], writes=[rbank[b]])
                    evac_copy(dstT[:, g:g + ng, :], banks16[b][:, 0:ng * 128].rearrange("p (k n) -> p k n", n=128),
                              reads=[rbank[b]], writes=[r_dst])
                bA = next_bank(); bB = next_bank()
                for k in range(KB):
                    S.op("pe", lambda e, k=k, bA=bA, dstT=dstT: e.matmul(banks[bA][:, 0:KVR], lhsT=dstT[:, k, :], rhs=wkl[:, k, 0:KVR],
                                                                        start=(k == 0), stop=(k == KB - 1)),
                         reads=[r_dst, r_wkl], writes=[rbank[bA]])
                for k in range(KB):
                    S.op("pe", lambda e, k=k, bB=bB, dstT=dstT: e.matmul(banks[bB][:, 0:ROPE], lhsT=dstT[:, k, :], rhs=wkl[:, k, KVR:KLW],
                                                                        start=(k == 0), stop=(k == KB - 1)),
                         reads=[r_dst, r_wkl], writes=[rbank[bB]])
                S.op("dve", lambda e, i=i, bA=bA: e.tensor_tensor(out=klf[i][:, 0:KVR], in0=banks[bA][:, 0:KVR], in1=bklb[:, 0:KVR], op=ALU.add),
                     reads=[rbank[bA], r_bkl], writes=[r_klf[i]])
                S.op("dve", lambda e, i=i, bB=bB: e.tensor_tensor(out=klf[i][:, KVR:KLW], in0=banks[bB][:, 0:ROPE], in1=bklb[:, KVR:KLW], op=ALU.add),
                     reads=[rbank[bB], r_bkl], writes=[r_klf[i]])
                dma("sp", cs_t[i][:, 0:32], tm_cos[trow:trow + 128, :], writes=[r_cs[i]])
                dma("sp", cs_t[i][:, 32:64], tm_sin[trow:trow + 128, :], writes=[r_cs[i]])
                S.op("dve", lambda e, i=i: e.memset(st_t[i][:, 0:1], 0.0), writes=[r_st[i]])
                S.op("act", lambda e, i=i: e.activation(out=junk, in_=klf[i][:, 0:KVR], func=AF.Square, accum_out=st_t[i][:, 0:1]),
                     reads=[r_klf[i]], writes=[r_junk, r_st[i]])
                S.op("act", lambda e, i=i: e.activation(out=st_t[i][:, 1:2], in_=st_t[i][:, 0:1], func=AF.Sqrt, scale=1.0 / KVR, bias=RMS_EPS),
                     reads=[r_st[i]], writes=[r_st[i]])
                S.op("dve", lambda e, i=i: e.reciprocal(out=st_t[i][:, 2:3], in_=st_t[i][:, 1:2]), reads=[r_st[i]], writes=[r_st[i]])
                S.op("dve", lambda e, i=i: e.scalar_tensor_tensor(out=ckf[i][:, 0:KVR], in0=klf[i][:, 0:KVR], scalar=st_t[i][:, 2:3],
                                                                  in1=kvgb, op0=ALU.mult, op1=ALU.mult),
                     reads=[r_klf[i], r_st[i], r_kvg], writes=[r_ckf[i]])
                x1 = klf[i][:, KVR:KVR + 32]; x2 = klf[i][:, KVR + 32:KLW]
                co = cs_t[i][:, 0:32]; si = cs_t[i][:, 32:64]
                t = rt_t[i]
                S.op("dve", lambda e, x1=x1, co=co, t=t: e.tensor_tensor(out=t[:, 0:32], in0=x1, in1=co, op=ALU.mult), reads=[r_klf[i], r_cs[i]], writes=[r_rt[i]])
                S.op("dve", lambda e, x2=x2, si=si, t=t: e.tensor_tensor(out=t[:, 32:64], in0=x2, in1=si, op=ALU.mult), reads=[r_klf[i], r_cs[i]], writes=[r_rt[i]])
                S.op("dve", lambda e, x1=x1, si=si, t=t: e.tensor_tensor(out=t[:, 64:96], in0=x1, in1=si, op=ALU.mult), reads=[r_klf[i], r_cs[i]], writes=[r_rt[i]])
                S.op("dve", lambda e, x2=x2, co=co, t=t: e.tensor_tensor(out=t[:, 96:128], in0=x2, in1=co, op=ALU.mult), reads=[r_klf[i], r_cs[i]], writes=[r_rt[i]])
                S.op("dve", lambda e, i=i, t=t: e.tensor_tensor(out=ckf[i][:, KVR:KVR + 32], in0=t[:, 0:32], in1=t[:, 32:64], op=ALU.subtract), reads=[r_rt[i]], writes=[r_ckf[i]])
                S.op("dve", lambda e, i=i, t=t: e.tensor_tensor(out=ckf[i][:, KVR + 32:KLW], in0=t[:, 64:96], in1=t[:, 96:128], op=ALU.add), reads=[r_rt[i]], writes=[r_ckf[i]])
                S.op("act", lambda e, i=i: e.activation(out=cbf[i], in_=ckf[i], func=AF.Copy), reads=[r_ckf[i]], writes=[r_cbf[i]])
                if orow is not None and "o" not in cfg.get("skip", ""):
                    dma("sp", ckv_out[orow:orow + 128, :], ckf[i][:, 0:KVR], reads=[r_ckf[i]])
                    dma("sp", kr_out[orow:orow + 128, :], ckf[i][:, KVR:KLW], reads=[r_ckf[i]])
                latent_tail(i, key_cols, orow)

        if "A" in cfg.get("stages", "ABCDEFG"):
            stage_A()

        def stage_B():
            S.barrier()
            state["top"] = PERM_TOP + (KB * T) // 2 + (KB * 32) // 2 + 16
            for r in rbank:
                r.w = None; r.rd = {}
            TOPB = state["top"]
            bga = A32([128, CB]); bgb = A32([128, CB]); wdwt = A32([128, CB, CW]); bdwt = A32([128, CB])
            hval = A32([128, 1])
            r_p = [load_small(bga, b_ga[:, :]), load_small(bgb, b_gb[:, :]),
                   load_small(wdwt, wdw.rearrange("p (c k) -> p c k", k=CW)), load_small(bdwt, bdw[:, :]),
                   load_small(hval, halo_valid[:, :])]
            histT = A32([128, CB, 64]); r_hist = R()
            tailsT = A32([128, CB, 96]); r_tails = R()
            S.op("dve", lambda e: e.memset(tailsT, 0.0), writes=[r_tails])
            pan = [A16([128, KB, 512]) for _ in range(2)]; r_pan = [R(), R()]
            TOPB1 = state["top"]
            hrowf = A32([64, CD]); r_hrf = R()
            dma("sp", hrowf, state_conv[:, :], writes=[r_hrf])
            for g in range(0, CB, 8):
                b = next_bank(); ng = min(8, CB - g)
                for c in range(g, g + ng):
                    S.op("pe", lambda e, b=b, c=c, g=g: e.transpose(banks[b][:, (c - g) * 64:(c - g + 1) * 64],
                                                                   hrowf[:, c * 128:(c + 1) * 128], ident_f[0:64, 0:64]),
                         reads=[r_hrf, r_const], writes=[rbank[b]])
                evac_copy(histT[:, g:g + ng, :], banks[b][:, 0:ng * 64].rearrange("p (k n) -> p k n", n=64),
                          reads=[rbank[b]], writes=[r_hist])
            S.barrier()
            state["top"] = TOPB1
            full = [A32([128, 1248]) for _ in range(2)]; r_full = [R(), R()]
            sigt = [A32([128, 512]) for _ in range(2)]; r_sig = [R(), R()]
            acc = [A32([128, T]) for _ in range(2)]; r_acc = [R(), R()]
            acc2 = [A32([128, T]) for _ in range(2)]; r_acc2 = [R(), R()]
            ctmp = [A32([128, T])]; r_ctmp = [R()]
            KD = cfg.get("KD", CW)
            ybf = [A16([128, T]) for _ in range(2)]; r_ybf = [R(), R()]
            S.op("dve", lambda e: e.memset(full[0], 0.0), writes=[r_full[0]])
            S.op("dve", lambda e: e.memset(full[1], 0.0), writes=[r_full[1]])
            r_scrB = R()
            xT_all_r = r_xT + [r_xTh]
            npan = (2 * CD) // 512
            dma("pool", pan[0], w_glu[:, 0:512].rearrange("(k p) c -> p k c", p=128), writes=[r_pan[0]])
            for pi in range(npan):
                pb = pi % 2
                if pi + 1 < npan:
                    dma("pool", pan[(pi + 1) % 2], w_glu[:, (pi + 1) * 512:(pi + 2) * 512].rearrange("(k p) c -> p k c", p=128),
                        writes=[r_pan[(pi + 1) % 2]])
                for j in range(2):
                    cb = 2 * pi + j; fb = cb % 2
                    fl = full[fb]
                    for ti, (t0, n) in enumerate(TT):
                        bA = next_bank(); bB = next_bank()
                        groups = [(0, n, xT[:, :, t0:t0 + n])]
                        if ti == 2:
                            groups.append((128, 32, xTh))
                        for (c0, w, rhs) in groups:
                            for (bk, off) in ((bA, j * 128), (bB, 256 + j * 128)):
                                for k in range(KB):
                                    S.op("pe", lambda e, bk=bk, off=off, k=k, c0=c0, w=w, rhs=rhs, pb=pb:
                                         e.matmul(banks[bk][:, c0:c0 + w], lhsT=pan[pb][:, k, off:off + 128], rhs=rhs[:, k, :],
                                                  start=(k == 0), stop=(k == KB - 1)),
                                         reads=[r_pan[pb]] + xT_all_r, writes=[rbank[bk]])
                        wtot = n if ti < 2 else 160
                        S.op("act", lambda e, bB=bB, fb=fb, wtot=wtot, cb=cb: e.activation(out=sigt[fb][:, 0:wtot], in_=banks[bB][:, 0:wtot],
                                                                                     func=AF.Sigmoid, bias=bgb[:, cb:cb + 1]),
                             reads=[rbank[bB]] + r_p, writes=[r_sig[fb]])
                        if ti < 2:
                            dsts = [(fl[:, 32 + t0:32 + t0 + n], 0, n)]
                        else:
                            dsts = [(fl[:, 1088:1152], 0, 64), (fl[:, 1184:1248], 64, 64), (fl[:, 0:32], 128, 32)]
                        for (dst, c0, w) in dsts:
                            S.op("dve", lambda e, dst=dst, c0=c0, w=w, bA=bA, fb=fb, cb=cb:
                                 e.scalar_tensor_tensor(out=dst, in0=banks[bA][:, c0:c0 + w], scalar=bga[:, cb:cb + 1],
                                                        in1=sigt[fb][:, c0:c0 + w], op0=ALU.add, op1=ALU.mult),
                                 reads=[rbank[bA], r_sig[fb]] + r_p, writes=[r_full[fb]])
                        if ti == 2:
                            S.op("dve", lambda e, fl=fl: e.tensor_scalar(out=fl[:, 0:32], in0=fl[:, 0:32], scalar1=hval[:, 0:1], scalar2=None, op0=ALU.mult),
                                 reads=r_p, writes=[r_full[fb]])
                    S.op("act", lambda e, fl=fl, cb=cb: e.activation(out=fl[:, 1058:1088], in_=histT[:, cb, 0:30], func=AF.Copy), reads=[r_hist], writes=[r_full[fb]])
                    S.op("act", lambda e, fl=fl, cb=cb: e.activation(out=fl[:, 1154:1184], in_=histT[:, cb, 32:62], func=AF.Copy), reads=[r_hist], writes=[r_full[fb]])
                    ac = acc[fb]; ac2 = acc2[fb]
                    segs = [(ac[:, 0:OWN], ac2[:, 0:OWN], lambda k, fl=fl: fl[:, 2 + k:2 + k + OWN]),
                            (ac[:, OWN:T].rearrange("p (s n) -> p s n", n=NS), ac2[:, OWN:T].rearrange("p (s n) -> p s n", n=NS),
                             lambda k, fl=fl: fl[:, 1056:1248].rearrange("p (s n) -> p s n", n=96)[:, :, 2 + k:2 + k + NS])]
                    for (oseg, o2seg, inf) in segs:
                        S.op("dve", lambda e, oseg=oseg, inf=inf, cb=cb: e.tensor_scalar(out=oseg, in0=inf(0), scalar1=wdwt[:, cb, 0:1], scalar2=bdwt[:, cb:cb + 1],
                                                                                     op0=ALU.mult, op1=ALU.add),
                             reads=[r_full[fb]] + r_p, writes=[r_acc[fb]])
                        for k in range(1, KD):
                            S.op("dve", lambda e, oseg=oseg, inf=inf, cb=cb, k=k: e.scalar_tensor_tensor(out=oseg, in0=inf(k), scalar=wdwt[:, cb, k:k + 1], in1=oseg,
                                                                                                   op0=ALU.mult, op1=ALU.add),
                                 reads=[r_full[fb]], writes=[r_acc[fb]])
                        if KD < CW:
                            S.op("act", lambda e, o2seg=o2seg, inf=inf, cb=cb: e.activation(out=o2seg, in_=inf(KD), func=AF.Identity, scale=wdwt[:, cb, KD:KD + 1]),
                                 reads=[r_full[fb]] + r_p, writes=[r_acc2[fb]])
                            for k in range(KD + 1, CW):
                                tq = 0
                                tseg = ctmp[tq][:, 0:OWN] if oseg is segs[0][0] else ctmp[tq][:, OWN:T].rearrange("p (s n) -> p s n", n=NS)
                                S.op("act", lambda e, tseg=tseg, inf=inf, cb=cb, k=k: e.activation(out=tseg, in_=inf(k), func=AF.Identity, scale=wdwt[:, cb, k:k + 1]),
                                     reads=[r_full[fb]] + r_p, writes=[r_ctmp[tq]])
                                S.op("pool", lambda e, o2seg=o2seg, tseg=tseg: e.tensor_tensor(out=o2seg, in0=o2seg, in1=tseg, op=ALU.add),
                                     reads=[r_ctmp[tq]], writes=[r_acc2[fb]])
                            S.op("dve", lambda e, oseg=oseg, o2seg=o2seg: e.tensor_tensor(out=oseg, in0=oseg, in1=o2seg, op=ALU.add),
                                 reads=[r_acc2[fb]], writes=[r_acc[fb]])
                    for (d0, s0) in ((0, 1026), (32, 1122), (64, 1218)):
                        S.op("act", lambda e, d0=d0, s0=s0, fl=fl, cb=cb: e.activation(out=tailsT[:, cb, d0:d0 + 30], in_=fl[:, s0:s0 + 30], func=AF.Copy),
                             reads=[r_full[fb]], writes=[r_tails])
                    S.op("act", lambda e, fb=fb: e.activation(out=ybf[fb], in_=acc[fb], func=AF.Copy), reads=[r_acc[fb]], writes=[r_ybf[fb]])
                    dma("sp", YT[cb * 128:(cb + 1) * 128, :], ybf[fb], reads=[r_ybf[fb]], writes=[r_scrB])
            csst = [A32([96, 512])] * 2; r_csst = [R()] * 2
            for g in range(0, CB, 4):
                b = next_bank(); ng = min(4, CB - g); q = (g // 4) % 2
                for c in range(g, g + ng):
                    S.op("pe", lambda e, b=b, c=c, g=g: e.transpose(banks[b][0:96, (c - g) * 128:(c - g + 1) * 128], tailsT[:, c, :], ident_f),
                         reads=[r_tails, r_const], writes=[rbank[b]])
                evac_copy(csst[q][:, 0:ng * 128], banks[b][0:96, 0:ng * 128], reads=[rbank[b]], writes=[r_csst[q]])
                dma("sp", cs_out[:, g * 128:(g + ng) * 128], csst[q][:, 0:ng * 128], reads=[r_csst[q]])

            S.barrier()
            state["top"] = TOPB1
            bqlt = A32([128, QB]); qagt = A32([128, QB])
            r_p2 = [load_small(bqlt, b_ql[:, :]), load_small(qagt, qag[:, :])]
            qlT = A16([128, QB, T]); r_ql = [R() for _ in range(QB)]
            sqt = [A16([128, T]) for _ in range(2)]; r_sq = [R(), R()]
            rstd = A32([128, T]); r_rstd = R()
            rs_t = A32([128, 512]); r_rs = R()
            for pi in range((QR + 511) // 512):
                pb = pi % 2; pw = min(512, QR - pi * 512)
                dma("pool", pan[pb][:, :, 0:pw], w_ql[:, pi * 512:pi * 512 + pw].rearrange("(k p) c -> p k c", p=128), writes=[r_pan[pb]])
                for j in range(pw // 128):
                    mb = pi * 4 + j
                    for ti, (t0, n) in enumerate(TT):
                        b = next_bank(0, 5)
                        for k in range(KB):
                            S.op("pe", lambda e, b=b, k=k, j=j, t0=t0, n=n, pb=pb: e.matmul(banks[b][:, 0:n], lhsT=pan[pb][:, k, j * 128:(j + 1) * 128],
                                                                                      rhs=xT[:, k, t0:t0 + n], start=(k == 0), stop=(k == KB - 1)),
                                 reads=[r_pan[pb]] + r_xT, writes=[rbank[b]])
                        S.op("act", lambda e, b=b, mb=mb, t0=t0, n=n: e.activation(out=qlT[:, mb, t0:t0 + n], in_=banks[b][:, 0:n], func=AF.Identity,
                                                                             bias=bqlt[:, mb:mb + 1]),
                             reads=[rbank[b]] + r_p2, writes=[r_ql[mb]])
            for mb in range(QB):
                q = mb % 2
                S.op("act", lambda e, mb=mb, q=q: e.activation(out=sqt[q], in_=qlT[:, mb, :], func=AF.Square), reads=[r_ql[mb]], writes=[r_sq[q]])
                for ti, (t0, n) in enumerate(TT):
                    S.op("pe", lambda e, ti=ti, t0=t0, n=n, q=q, mb=mb: e.matmul(banks[5 + ti][:, 0:n], lhsT=ones_b, rhs=sqt[q][:, t0:t0 + n],
                                                                           start=(mb == 0), stop=(mb == QB - 1)),
                         reads=[r_sq[q], r_const], writes=[rbank[5 + ti]])
            for ti, (t0, n) in enumerate(TT):
                S.op("act", lambda e, ti=ti, n=n: e.activation(out=rs_t[:, 0:n], in_=banks[5 + ti][:, 0:n], func=AF.Sqrt, scale=1.0 / QR, bias=RMS_EPS),
                     reads=[rbank[5 + ti]], writes=[r_rs])
                S.op("dve", lambda e, t0=t0, n=n: e.reciprocal(out=rstd[:, t0:t0 + n], in_=rs_t[:, 0:n]), reads=[r_rs], writes=[r_rstd])
            for mb in range(QB):
                S.op("dve", lambda e, mb=mb: e.scalar_tensor_tensor(out=qlT[:, mb, :], in0=qlT[:, mb, :], scalar=qagt[:, mb:mb + 1], in1=rstd,
                                                                    op0=ALU.mult, op1=ALU.mult),
                     reads=[r_rstd] + r_p2, writes=[r_ql[mb]])
            dma("sp", QNT.rearrange("(k p) t -> p k t", p=128), qlT, reads=r_ql, writes=[r_scrB])

            S.barrier()
            state["top"] = TOPB1
            for r in rbank:
                r.w = None; r.rd = {}
            bgt = A32([128, 2 * KB]); r_p4 = [load_small(bgt, b_gate[:, :])]
            sgst = [A16([128, T]) for _ in range(2)]; r_sgst = [R(), R()]
            for pi in range((2 * D) // 512):
                pb = pi % 2
                dma("pool", pan[pb], w_gate[:, pi * 512:(pi + 1) * 512].rearrange("(k p) c -> p k c", p=128), writes=[r_pan[pb]])
                for j in range(4):
                    mb = pi * 4 + j; q = mb % 2
                    for ti, (t0, n) in enumerate(TT):
                        b = next_bank()
                        for k in range(KB):
                            S.op("pe", lambda e, b=b, k=k, j=j, t0=t0, n=n, pb=pb: e.matmul(banks[b][:, 0:n], lhsT=pan[pb][:, k, j * 128:(j + 1) * 128],
                                                                                      rhs=xT[:, k, t0:t0 + n], start=(k == 0), stop=(k == KB - 1)),
                                 reads=[r_pan[pb]] + r_xT, writes=[rbank[b]])
                        S.op("act", lambda e, b=b, mb=mb, t0=t0, n=n, q=q: e.activation(out=sgst[q][:, t0:t0 + n], in_=banks[b][:, 0:n], func=AF.Sigmoid,
                                                                                  bias=bgt[:, mb:mb + 1]),
                             reads=[rbank[b]] + r_p4, writes=[r_sgst[q]])
                    dma("sp", SGT[mb * 128:(mb + 1) * 128, :], sgst[q], reads=[r_sgst[q]], writes=[r_scrB])

        if "B" in cfg.get("stages", "ABCDEFG"):
            stage_B()

        def stage_C():
            new_stage()
            cngt = A32([128, CB]); cnbt = A32([128, CB])
            r_pc = [load_small(cngt, cng[:, :]), load_small(cnbt, cnb[:, :])]
            zT = A16([128, CB, T]); r_z = [R() for _ in range(CB)]
            for c in range(CB):
                dma("sp", zT[:, c, :], YT[c * 128:(c + 1) * 128, :], writes=[r_z[c]])
            pan = [A16([128, KB, 512]) for _ in range(2)]; r_pan = [R(), R()]
            sqt = [A16([128, T]) for _ in range(2)]; r_sq = [R(), R()]
            mean = A32([128, T]); rstd = A32([128, T]); nmr = A32([128, T]); r_stat = R()
            tmpc = [A32([128, T]) for _ in range(2)]; r_tmpc = [R(), R()]
            for c in range(CB):
                q = c % 2
                for ti, (t0, n) in enumerate(TT):
                    S.op("pe", lambda e, ti=ti, t0=t0, n=n, c=c: e.matmul(banks[ti][:, 0:n], lhsT=ones_b, rhs=zT[:, c, t0:t0 + n],
                                                                      start=(c == 0), stop=(c == CB - 1)),
                         reads=[r_z[c], r_const], writes=[rbank[ti]])
                S.op("act", lambda e, c=c, q=q: e.activation(out=sqt[q], in_=zT[:, c, :], func=AF.Square), reads=[r_z[c]], writes=[r_sq[q]])
                for ti, (t0, n) in enumerate(TT):
                    S.op("pe", lambda e, ti=ti, t0=t0, n=n, c=c, q=q: e.matmul(banks[3 + ti][:, 0:n], lhsT=ones_b, rhs=sqt[q][:, t0:t0 + n],
                                                                           start=(c == 0), stop=(c == CB - 1)),
                         reads=[r_sq[q], r_const], writes=[rbank[3 + ti]])
            for ti, (t0, n) in enumerate(TT):
                sl = slice(t0, t0 + n)
                S.op("act", lambda e, ti=ti, n=n, sl=sl: e.activation(out=mean[:, sl], in_=banks[ti][:, 0:n], func=AF.Identity, scale=1.0 / CD),
                     reads=[rbank[ti]], writes=[r_stat])
                S.op("dve", lambda e, sl=sl: e.tensor_tensor(out=nmr[:, sl], in0=mean[:, sl], in1=mean[:, sl], op=ALU.mult), reads=[r_stat], writes=[r_stat])
                S.op("dve", lambda e, ti=ti, n=n, sl=sl: e.scalar_tensor_tensor(out=rstd[:, sl], in0=banks[3 + ti][:, 0:n], scalar=1.0 / CD, in1=nmr[:, sl],
                                                                            op0=ALU.mult, op1=ALU.subtract),
                     reads=[rbank[3 + ti], r_stat], writes=[r_stat])
                S.op("act", lambda e, sl=sl: e.activation(out=rstd[:, sl], in_=rstd[:, sl], func=AF.Sqrt, bias=LN_EPS), reads=[r_stat], writes=[r_stat])
                S.op("dve", lambda e, sl=sl: e.reciprocal(out=rstd[:, sl], in_=rstd[:, sl]), reads=[r_stat], writes=[r_stat])
                S.op("dve", lambda e, sl=sl: e.scalar_tensor_tensor(out=nmr[:, sl], in0=mean[:, sl], scalar=-1.0, in1=rstd[:, sl], op0=ALU.mult, op1=ALU.mult),
                     reads=[r_stat], writes=[r_stat])
            for c in range(CB):
                q = c % 2
                S.op("dve", lambda e, c=c, q=q: e.tensor_tensor(out=tmpc[q], in0=zT[:, c, :], in1=rstd, op=ALU.mult), reads=[r_z[c], r_stat], writes=[r_tmpc[q]])
                S.op("dve", lambda e, q=q: e.tensor_tensor(out=tmpc[q], in0=tmpc[q], in1=nmr, op=ALU.add), reads=[r_stat], writes=[r_tmpc[q]])
                S.op("act", lambda e, c=c, q=q: e.activation(out=zT[:, c, :], in_=tmpc[q], func=AF.Silu, scale=cngt[:, c:c + 1], bias=cnbt[:, c:c + 1]),
                     reads=[r_tmpc[q]] + r_pc, writes=[r_z[c]])
            sgb = [A16([128, T]) for _ in range(2)]; r_sgb = [R(), R()]
            m1st = [A32([128, T]) for _ in range(2)]; r_m1st = [R(), R()]
            r_scrC = R()
            for pi in range(D // 512):
                pb = pi % 2
                dma("pool", pan[pb], w_pw[:, pi * 512:(pi + 1) * 512].rearrange("(k p) c -> p k c", p=128), writes=[r_pan[pb]])
                for j in range(4):
                    mb = pi * 4 + j; q = mb % 2
                    dma("sp", sgb[q], SGT[mb * 128:(mb + 1) * 128, :], writes=[r_sgb[q]])
                    for ti, (t0, n) in enumerate(TT):
                        b = 6 + next_bank(0, 2)
                        for k in range(CB):
                            S.op("pe", lambda e, b=b, k=k, j=j, t0=t0, n=n, pb=pb: e.matmul(banks[b][:, 0:n], lhsT=pan[pb][:, k, j * 128:(j + 1) * 128],
                                                                                      rhs=zT[:, k, t0:t0 + n], start=(k == 0), stop=(k == CB - 1)),
                                 reads=[r_pan[pb]] + r_z, writes=[rbank[b]])
                        S.op("dve", lambda e, b=b, t0=t0, n=n, q=q: e.tensor_tensor(out=m1st[q][:, t0:t0 + n], in0=banks[b][:, 0:n], in1=sgb[q][:, t0:t0 + n], op=ALU.mult),
                             reads=[rbank[b], r_sgb[q]], writes=[r_m1st[q]])
                    dma("sp", M1T[mb * 128:(mb + 1) * 128, :], m1st[q], reads=[r_m1st[q]], writes=[r_scrC])

        if "C" in cfg.get("stages", "ABCDEFG"):
            stage_C()

        def stage_D():
            new_stage()
            NKB = NK // 128
            qnT = A16([128, QB, T]); r_qn = R()
            dma("sp", qnT, QNT.rearrange("(k p) t -> p k t", p=128), writes=[r_qn])
            ckvT = A16([128, RB, NK]); r_ckv = R()
            for r in range(RB):
                dma("sp", ckvT[:, r, :], CKVT[r * 128:(r + 1) * 128, :], writes=[r_ckv])
            krT = A16([64, NK]); r_kr = R()
            dma("sp", krT, KRT[:, :], writes=[r_kr])
            fct = A32([64, T]); fst = A32([64, T]); kbt = A32([128, NCTX // 128])
            r_pd = [load_small(fct, fm_c[:, :]), load_small(fst, fm_s[:, :]), load_small(kbt, keybias[:, :])]
            wq = [A16([128, QB, 256]) for _ in range(2)]; r_wq = [R(), R()]
            wkv = [A16([128, RB, 256]) for _ in range(2)]; r_wkv = [R(), R()]
            Qn = A16([128, T]); r_Qn = R()
            Qr = A16([64, T]); r_Qr = R()
            Kh = A16([128, NK]); r_Kh = R()
            Vh = A16([128, NKB, 128]); r_Vh = R()
            t1 = [A32([64, 512]) for _ in range(2)]; r_t1 = [R(), R()]
            t2 = [A32([64, 512]) for _ in range(2)]; r_t2 = [R(), R()]
            NPB = 6
            Pb = [A16([128, 512]) for _ in range(NPB)]; r_Pb = [R() for _ in range(NPB)]
            Pd = [A16([128, 512]) for _ in range(4)]; r_Pd = [R() for _ in range(4)]
            for d in range(4):
                S.op("dve", lambda e, d=d: e.memset(Pd[d], 0.0), writes=[r_Pd[d]])
            rden = [A32([128, 512]) for _ in range(2)]; r_rden = [R(), R()]
            dacc = [A32([128, 512]) for _ in range(2)]; r_dacc = [R(), R()]
            dacc2 = [A32([128, 512]) for _ in range(2)]; r_dacc2 = [R(), R()]
            Ost = [A16([128, T]) for _ in range(2)]; r_Ost = [R(), R()]
            r_scrD = R()
            pstate = {"p": 0, "u": 0}
            for h in range(H):
                hb = h % 2
                def load_head_w(h_):
                    dma("pool", wq[h_ % 2], w_q[:, h_ * 256:(h_ + 1) * 256].rearrange("(k p) c -> p k c", p=128), writes=[r_wq[h_ % 2]])
                    dma("pool", wkv[h_ % 2], w_kv[:, h_ * 256:(h_ + 1) * 256].rearrange("(k p) c -> p k c", p=128), writes=[r_wkv[h_ % 2]])
                if h == 0:
                    load_head_w(0)
                for ti, (t0, n) in enumerate(TT):
                    b = 6 + next_bank(0, 2)
                    for k in range(QB):
                        S.op("pe", lambda e, b=b, k=k, t0=t0, n=n, hb=hb: e.matmul(banks[b][:, 0:n], lhsT=wq[hb][:, k, 0:128], rhs=qnT[:, k, t0:t0 + n],
                                                                             start=(k == 0), stop=(k == QB - 1)),
                             reads=[r_wq[hb], r_qn], writes=[rbank[b]])
                    evac_copy(Qn[:, t0:t0 + n], banks[b][:, 0:n], reads=[rbank[b]], writes=[r_Qn])
                    bA = 6 + next_bank(0, 2)
                    for k in range(QB):
                        S.op("pe", lambda e, b=bA, k=k, t0=t0, n=n, hb=hb: e.matmul(banks[b][0:64, 0:n], lhsT=wq[hb][:, k, 128:192], rhs=qnT[:, k, t0:t0 + n],
                                                                              start=(k == 0), stop=(k == QB - 1)),
                             reads=[r_wq[hb], r_qn], writes=[rbank[bA]])
                    q = ti % 2
                    S.op("dve", lambda e, b=bA, t0=t0, n=n, q=q: e.tensor_tensor(out=t1[q][:, 0:n], in0=banks[b][0:64, 0:n], in1=fct[:, t0:t0 + n], op=ALU.mult),
                         reads=[rbank[bA]] + r_pd, writes=[r_t1[q]])
                    bB = 6 + next_bank(0, 2)
                    for k in range(QB):
                        S.op("pe", lambda e, b=bB, k=k, t0=t0, n=n, hb=hb: e.matmul(banks[b][0:64, 0:n], lhsT=wq[hb][:, k, 192:256], rhs=qnT[:, k, t0:t0 + n],
                                                                              start=(k == 0), stop=(k == QB - 1)),
                             reads=[r_wq[hb], r_qn], writes=[rbank[bB]])
                    S.op("dve", lambda e, b=bB, t0=t0, n=n, q=q: e.tensor_tensor(out=t2[q][:, 0:n], in0=banks[b][0:64, 0:n], in1=fst[:, t0:t0 + n], op=ALU.mult),
                         reads=[rbank[bB]] + r_pd, writes=[r_t2[q]])
                    S.op("dve", lambda e, t0=t0, n=n, q=q: e.tensor_tensor(out=Qr[:, t0:t0 + n], in0=t1[q][:, 0:n], in1=t2[q][:, 0:n], op=ALU.add),
                         reads=[r_t1[q], r_t2[q]], writes=[r_Qr])
                for k0 in range(0, NK, 512):
                    n = min(512, NK - k0)
                    b = 6 + next_bank(0, 2)
                    for r in range(RB):
                        S.op("pe", lambda e, b=b, r=r, k0=k0, n=n, hb=hb: e.matmul(banks[b][:, 0:n], lhsT=wkv[hb][:, r, 0:128], rhs=ckvT[:, r, k0:k0 + n],
                                                                             start=(r == 0), stop=(r == RB - 1)),
                             reads=[r_wkv[hb], r_ckv], writes=[rbank[b]])
                    evac_copy(Kh[:, k0:k0 + n], banks[b][:, 0:n], reads=[rbank[b]], writes=[r_Kh])
                for g in range(0, NKB, 4):
                    ng = min(4, NKB - g)
                    b = 6 + next_bank(0, 2)
                    for kb in range(g, g + ng):
                        for r in range(RB):
                            S.op("pe", lambda e, b=b, r=r, kb=kb, g=g, hb=hb: e.matmul(banks[b][:, (kb - g) * 128:(kb - g + 1) * 128],
                                                                                 lhsT=ckvT[:, r, kb * 128:(kb + 1) * 128], rhs=wkv[hb][:, r, 128:256],
                                                                                 start=(r == 0), stop=(r == RB - 1)),
                                 reads=[r_wkv[hb], r_ckv], writes=[rbank[b]])
                    evac_copy(Vh[:, g:g + ng, :], banks[b][:, 0:ng * 128].rearrange("p (k n) -> p k n", n=128), reads=[rbank[b]], writes=[r_Vh])
                oq = h % 2
                units = []
                for qt in range(2):
                    kl = [(kb, 128, ("ctx", kb)) for kb in range(NCTX // 128)]
                    for ob in range(4 * (qt + 1)):
                        kl.append((NCTX // 128 + ob, 128, ("full",) if ob < 4 * qt else ("diag", ob - 4 * qt)))
                    units.append((qt * 512, 512, kl))
                for s in range(2):
                    base = (NCTX + OWN) // 128 + s * (NKS // 128)
                    kl = [(base + i, 128, ("full",)) for i in range(PAST // 128)] + [(base + PAST // 128, NS, ("full",))]
                    units.append((OWN + s * NS, NS, kl))
                def run_unit(q0, nq, kl, oq):
                    u = pstate["u"]; pstate["u"] += 1
                    bO = 3 + (u % 2); bD = 5
                    pend = None
                    nkl = len(kl)

                    da = dacc[u % 2]; r_da = r_dacc[u % 2]
                    da2 = dacc2[u % 2]; r_da2 = r_dacc2[u % 2]
                    nadd = {"n": 0}

                    def issue_pv(pp, first, last):
                        (Pap, rP, kp, kb) = pp
                        S.op("pe", lambda e: e.matmul(banks[bO][:, 0:nq], lhsT=Vh[0:kp, kb, :], rhs=Pap[0:kp, 0:nq], start=first, stop=last),
                             reads=[rP, r_Vh], writes=[rbank[bO]])
                        i_ = nadd["n"]; nadd["n"] += 1
                        eng_, acc_, racc_ = ("dve", da, r_da) if i_ % 2 == 0 else ("pool", da2, r_da2)
                        if i_ < 2:
                            S.op(eng_, lambda e: e.tensor_copy(out=acc_[0:kp, 0:nq], in_=Pap[0:kp, 0:nq]), reads=[rP], writes=[racc_])
                        else:
                            S.op(eng_, lambda e: e.tensor_tensor(out=acc_[0:kp, 0:nq], in0=acc_[0:kp, 0:nq], in1=Pap[0:kp, 0:nq], op=ALU.add),
                                 reads=[rP], writes=[racc_])
                        if last:
                            S.op("pe", lambda e: e.matmul(banks[bD][:, 0:nq], lhsT=ones_f, rhs=da[:, 0:nq], start=True, stop=False),
                                 reads=[r_da, r_const], writes=[rbank[bD]])
                            S.op("pe", lambda e: e.matmul(banks[bD][:, 0:nq], lhsT=ones_f, rhs=da2[:, 0:nq], start=False, stop=True),
                                 reads=[r_da2, r_const], writes=[rbank[bD]])

                    for idx, (kb, kp, kind) in enumerate(kl):
                        bS = next_bank(0, 3)
                        S.op("pe", lambda e, bS=bS, kb=kb, kp=kp: e.matmul(banks[bS][0:kp, 0:nq], lhsT=Kh[:, kb * 128:kb * 128 + kp], rhs=Qn[:, q0:q0 + nq],
                                                                         start=True, stop=False),
                             reads=[r_Kh, r_Qn], writes=[rbank[bS]])
                        S.op("pe", lambda e, bS=bS, kb=kb, kp=kp: e.matmul(banks[bS][0:kp, 0:nq], lhsT=krT[:, kb * 128:kb * 128 + kp], rhs=Qr[:, q0:q0 + nq],
                                                                         start=False, stop=True),
                             reads=[r_kr, r_Qr], writes=[rbank[bS]])
                        if kind[0] == "diag":
                            d = kind[1]
                            Pap = Pd[d]; rP = r_Pd[d]
                            if 128 * d + 64 < 512:
                                S.op("act", lambda e, bS=bS, d=d, Pap=Pap: e.activation(out=Pap[:, 128 * d + 64:512], in_=banks[bS][:, 128 * d + 64:512],
                                                                                    func=AF.Exp, scale=SCALE),
                                     reads=[rbank[bS]], writes=[rP])
                            S.op("act", lambda e, bS=bS, d=d, Pap=Pap: e.activation(out=Pap[0:64, 128 * d:128 * d + 64], in_=banks[bS][0:64, 128 * d:128 * d + 64],
                                                                                func=AF.Exp, scale=SCALE),
                                 reads=[rbank[bS]], writes=[rP])
                        else:
                            pi_ = pstate["p"] % NPB; pstate["p"] += 1
                            Pap = Pb[pi_]; rP = r_Pb[pi_]
                            if kind[0] == "ctx":
                                S.op("act", lambda e, bS=bS, kp=kp, Pap=Pap, c=kind[1]: e.activation(out=Pap[0:kp, 0:nq], in_=banks[bS][0:kp, 0:nq], func=AF.Exp,
                                                                                                 scale=SCALE, bias=kbt[:, c:c + 1]),
                                     reads=[rbank[bS]] + r_pd, writes=[rP])
                            else:
                                S.op("act", lambda e, bS=bS, kp=kp, Pap=Pap: e.activation(out=Pap[0:kp, 0:nq], in_=banks[bS][0:kp, 0:nq], func=AF.Exp, scale=SCALE),
                                     reads=[rbank[bS]], writes=[rP])
                        if pend is not None:
                            issue_pv(pend[0], pend[1] == 0, False)
                        pend = ((Pap, rP, kp, kb), idx)
                    issue_pv(pend[0], pend[1] == 0, True)
                    rq = u % 2
                    S.op("dve", lambda e, rq=rq: e.reciprocal(out=rden[rq][:, 0:nq], in_=banks[bD][:, 0:nq]), reads=[rbank[bD]], writes=[r_rden[rq]])
                    S.op("dve", lambda e, rq=rq: e.tensor_tensor(out=Ost[oq][:, q0:q0 + nq], in0=banks[bO][:, 0:nq], in1=rden[rq][:, 0:nq], op=ALU.mult),
                         reads=[rbank[bO], r_rden[rq]], writes=[r_Ost[oq]])
                if h + 1 < H:
                    load_head_w(h + 1)
                for (q0_, nq_, kl_) in units:
                    run_unit(q0_, nq_, kl_, oq)
                dma("sp", OT[h * 128:(h + 1) * 128, :], Ost[oq], reads=[r_Ost[oq]], writes=[r_scrD])

        if "D" in cfg.get("stages", "ABCDEFG"):
            stage_D()

        def stage_E():
            new_stage()
            groupsE = [[(0, 512)], [(512, 512), (1024, 128)]]
            oT = A16([128, H, 640]); r_oT = R()
            panE = [A16([128, H, 256]) for _ in range(2)]; r_panE = [R(), R()]
            sgE = [A16([128, 512]) for _ in range(2)]; r_sgE = [R(), R()]
            m1E = [A32([128, 512]) for _ in range(2)]; r_m1E = [R(), R()]
            tE = [A32([128, 512]) for _ in range(2)]; r_tE = [R(), R()]
            mgE = [A16([128, 512]) for _ in range(2)]; r_mgE = [R(), R()]
            r_scrE = R()
            ecnt = 0
            for grp in groupsE:
                g0 = grp[0][0]; gn = sum(n for _, n in grp)
                for hh in range(H):
                    dma("sp", oT[:, hh, 0:gn], OT[hh * 128:(hh + 1) * 128, g0:g0 + gn], reads=[], writes=[r_oT])
                for pi in range(D // 256):
                    pb = pi % 2
                    dma("pool", panE[pb], w_o[:, pi * 256:(pi + 1) * 256].rearrange("(k p) c -> p k c", p=128), writes=[r_panE[pb]])
                    for j in range(2):
                        mb = pi * 2 + j
                        for (t0, n) in grp:
                            q = ecnt % 2; ecnt += 1
                            dma("sp", sgE[q][:, 0:n], SGT[(KB + mb) * 128:(KB + mb + 1) * 128, t0:t0 + n], writes=[r_sgE[q]])
                            dma("sp", m1E[q][:, 0:n], M1T[mb * 128:(mb + 1) * 128, t0:t0 + n], writes=[r_m1E[q]])
                            b = next_bank()
                            for k in range(H):
                                S.op("pe", lambda e, b=b, k=k, j=j, t0=t0, n=n, pb=pb, g0=g0: e.matmul(banks[b][:, 0:n], lhsT=panE[pb][:, k, j * 128:(j + 1) * 128],
                                                                                                 rhs=oT[:, k, t0 - g0:t0 - g0 + n], start=(k == 0), stop=(k == H - 1)),
                                     reads=[r_panE[pb], r_oT], writes=[rbank[b]])
                            S.op("dve", lambda e, b=b, n=n, q=q: e.tensor_tensor(out=tE[q][:, 0:n], in0=banks[b][:, 0:n], in1=sgE[q][:, 0:n], op=ALU.mult),
                                 reads=[rbank[b], r_sgE[q]], writes=[r_tE[q]])
                            S.op("dve", lambda e, n=n, q=q: e.tensor_tensor(out=mgE[q][:, 0:n], in0=tE[q][:, 0:n], in1=m1E[q][:, 0:n], op=ALU.add),
                                 reads=[r_tE[q], r_m1E[q]], writes=[r_mgE[q]])
                            dma("sp", MGT[mb * 128:(mb + 1) * 128, t0:t0 + n], mgE[q][:, 0:n], reads=[r_mgE[q]], writes=[r_scrE])

        if "E" in cfg.get("stages", "ABCDEFG"):
            stage_E()

        def stage_F():
            new_stage()
            NTB = T // 128
            mgT = A16([128, KB, T]); r_mg = R()
            for k in range(KB):
                dma("sp", mgT[:, k, :], MGT[k * 128:(k + 1) * 128, :], writes=[r_mg])
            pan = [A16([128, KB, 512]) for _ in range(2)]; r_pan = [R(), R()]
            xtl = [A32([128, 512]) for _ in range(2)]; r_xtl = [R(), R()]
            hpst = [A32([128, 512]) for _ in range(2)]; r_hpst = [R(), R()]
            r_scrF = R()
            fcnt = 0
            for pi in range(D // 512):
                pb = pi % 2
                dma("pool", pan[pb], w_out[:, pi * 512:(pi + 1) * 512].rearrange("(k p) c -> p k c", p=128), writes=[r_pan[pb]])
                for tb in range(NTB):
                    q = fcnt % 2; fcnt += 1
                    dma("sp", xtl[q], x_all[tb * 128:(tb + 1) * 128, pi * 512:(pi + 1) * 512], writes=[r_xtl[q]])
                    b = next_bank()
                    for k in range(KB):
                        S.op("pe", lambda e, b=b, k=k, tb=tb, pb=pb: e.matmul(banks[b], lhsT=mgT[:, k, tb * 128:(tb + 1) * 128], rhs=pan[pb][:, k, :],
                                                                        start=(k == 0), stop=(k == KB - 1)),
                             reads=[r_pan[pb], r_mg], writes=[rbank[b]])
                    S.op("dve", lambda e, b=b, q=q: e.scalar_tensor_tensor(out=hpst[q], in0=xtl[q], scalar=ALPHA, in1=banks[b], op0=ALU.mult, op1=ALU.add),
                         reads=[rbank[b], r_xtl[q]], writes=[r_hpst[q]])
                    dma("sp", HP[tb * 128:(tb + 1) * 128, pi * 512:(pi + 1) * 512], hpst[q], reads=[r_hpst[q]], writes=[r_scrF])
            new_stage()

            g1t = A32([128, D]); b1t = A32([128, D])
            r_g1 = [load_small(g1t, ln1g[0:1, :].partition_broadcast(128)), load_small(b1t, ln1b[0:1, :].partition_broadcast(128))]
            hpt = [A32([128, D]) for _ in range(2)]; r_hpt = [R(), R()]
            junkb = A16([128, D]); r_junkb = R()
            hbf = [A16([128, D]) for _ in range(2)]; r_hbf = [R(), R()]
            hTst = [A16([128, KB, 128]) for _ in range(2)]; r_hTst = [R(), R()]
            stF = [A32([128, 8]) for _ in range(2)]; r_stF = [R(), R()]
            for tb in range(NTB):
                q = tb % 2
                dma("sp", hpt[q], HP[tb * 128:(tb + 1) * 128, :], writes=[r_hpt[q]])
                layer_norm_rows(hpt[q], r_hpt[q], g1t, b1t, r_g1, junkb, r_junkb, stF[q], r_stF[q])
                dma("sp", HS[tb * 128:(tb + 1) * 128, :], hpt[q], reads=[r_hpt[q]], writes=[r_scrF])
                S.op("act", lambda e, q=q: e.activation(out=hbf[q], in_=hpt[q], func=AF.Copy), reads=[r_hpt[q]], writes=[r_hbf[q]])
                for g in range(0, KB, 8):
                    b = next_bank(); ng = min(8, KB - g)
                    for k in range(g, g + ng):
                        S.op("pe", lambda e, b=b, k=k, g=g, q=q: e.transpose(banks16[b][:, (k - g) * 128:(k - g + 1) * 128], hbf[q][:, k * 128:(k + 1) * 128], ident_b),
                             reads=[r_hbf[q], r_const], writes=[rbank[b]])
                    evac_copy(hTst[q][:, g:g + ng, :], banks16[b][:, 0:ng * 128].rearrange("p (k n) -> p k n", n=128), reads=[rbank[b]], writes=[r_hTst[q]])
                dma("sp", HT[:, tb * 128:(tb + 1) * 128].rearrange("(k p) n -> p k n", p=128), hTst[q], reads=[r_hTst[q]], writes=[r_scrF])

        if "F" in cfg.get("stages", "ABCDEFG"):
            stage_F()

        def stage_G():
            groupsG = [([0, 1, 2, 3, 8], [(0, 512), (1024, 128)]), ([4, 5, 6, 7], [(512, 512)])]
            NCH = DFF // 512
            def ffn_group(tbs, tiles):
                new_stage()
                gn = sum(n for _, n in tiles)
                hTg = A16([128, KB, gn]); r_hTg = R()
                c0 = 0
                loc = []
                for (t0, n) in tiles:
                    for k in range(KB):
                        dma("sp", hTg[:, k, c0:c0 + n], HT[k * 128:(k + 1) * 128, t0:t0 + n], writes=[r_hTg])
                    loc.append((c0, n)); c0 += n
                facc = A32([128, len(tbs), D]); r_facc = [[R() for _ in range(D // 512)] for _ in tbs]
                for i, tb in enumerate(tbs):
                    dma("sp", facc[:, i, :], HS[tb * 128:(tb + 1) * 128, :], writes=r_facc[i])
                    S.op("act", lambda e, i=i: e.activation(out=facc[:, i, :], in_=facc[:, i, :], func=AF.Identity, scale=ALPHA), reads=[], writes=r_facc[i])
                MARK = state["top"]
                wup = [A16([128, KB, 128]) for _ in range(3)]; r_wup = [R() for _ in range(3)]
                wdn = [A16([128, 4, 2048]) for _ in range(2)]; r_wdn = [R(), R()]
                aT = [A16([128, 4, gn]) for _ in range(2)]; r_aT = [R(), R()]
                rt = [A32([128, 512]) for _ in range(2)]; r_rt2 = [R(), R()]
                ucnt = 0; dcnt = 0; rcnt = 0
                nhalf = D // 2048 if D >= 2048 else 1
                hw = D // nhalf
                cnts = {"u": 0, "d": 0, "r": 0}
                def ffn_up(c):
                    ab = c % 2
                    for hbk in range(4):
                        hid = 4 * c + hbk
                        wb = cnts["u"] % 3; cnts["u"] += 1
                        dma("pool", wup[wb], w_up[:, hid * 128:(hid + 1) * 128].rearrange("(k p) c -> p k c", p=128), writes=[r_wup[wb]])
                        for (l0, n) in loc:
                            b = next_bank(0, 4)
                            for k in range(KB):
                                S.op("pe", lambda e, b=b, k=k, wb=wb, l0=l0, n=n: e.matmul(banks[b][:, 0:n], lhsT=wup[wb][:, k, :], rhs=hTg[:, k, l0:l0 + n],
                                                                                     start=(k == 0), stop=(k == KB - 1)),
                                     reads=[r_wup[wb], r_hTg], writes=[rbank[b]])
                            q = cnts["r"] % 2; cnts["r"] += 1
                            S.op("act", lambda e, b=b, n=n, q=q: e.activation(out=rt[q][:, 0:n], in_=banks[b][:, 0:n], func=AF.Relu), reads=[rbank[b]], writes=[r_rt2[q]])
                            S.op("dve", lambda e, n=n, q=q, ab=ab, hbk=hbk, l0=l0: e.tensor_tensor(out=aT[ab][:, hbk, l0:l0 + n], in0=rt[q][:, 0:n], in1=rt[q][:, 0:n], op=ALU.mult),
                                 reads=[r_rt2[q]], writes=[r_aT[ab]])
                def ffn_down(c):
                    ab = c % 2
                    for hf in range(nhalf):
                        db = cnts["d"] % 2; cnts["d"] += 1
                        dma("pool", wdn[db][:, :, 0:hw], w_down[c * 512:(c + 1) * 512, hf * hw:(hf + 1) * hw].rearrange("(k p) n -> p k n", p=128), writes=[r_wdn[db]])
                        for i in range(len(tbs)):
                            for nt in range(hw // 512):
                                b = 4 + next_bank(0, 4)
                                for hbk in range(4):
                                    S.op("pe", lambda e, b=b, hbk=hbk, i=i, nt=nt, db=db, ab=ab: e.matmul(banks[b], lhsT=aT[ab][:, hbk, i * 128:(i + 1) * 128],
                                                                                                    rhs=wdn[db][:, hbk, nt * 512:(nt + 1) * 512],
                                                                                                    start=(hbk == 0), stop=(hbk == 3)),
                                         reads=[r_wdn[db], r_aT[ab]], writes=[rbank[b]])
                                col = hf * hw + nt * 512
                                S.op("dve", lambda e, b=b, i=i, col=col: e.tensor_tensor(out=facc[:, i, col:col + 512], in0=banks[b], in1=facc[:, i, col:col + 512], op=ALU.add),
                                     reads=[rbank[b]], writes=[r_facc[i][col // 512]])
                ffn_up(0)
                for c_ in range(NCH):
                    if c_ + 1 < NCH:
                        ffn_up(c_ + 1)
                    ffn_down(c_)
                S.barrier()
                state["top"] = MARK
                g2t = A32([128, D]); b2t = A32([128, D])
                r_g2 = [load_small(g2t, ln2g[0:1, :].partition_broadcast(128)), load_small(b2t, ln2b[0:1, :].partition_broadcast(128))]
                junk2 = A16([128, D]); r_junk2 = R()
                stG = [A32([128, 8]) for _ in range(2)]; r_stG = [R(), R()]
                for i, tb in enumerate(tbs):
                    rr = R()
                    layer_norm_rows(facc[:, i, :], rr, g2t, b2t, r_g2, junk2, r_junk2, stG[i % 2], r_stG[i % 2])
                    dma("sp", y_out[tb * 128:(tb + 1) * 128, :], facc[:, i, :], reads=[rr])

            for (tbs_, tiles_) in groupsG:
                ffn_group(tbs_, tiles_)
        if "G" in cfg.get("stages", "ABCDEFG"):
            stage_G()

        S.emit()
    return nc


_CACHE = {}


def _prep_inputs(inp):
    f = lambda a: np.ascontiguousarray(np.asarray(a, dtype=np.float32))
    x_prompt = f(inp["x_prompt"]); x_sample = f(inp["x_sample"])
    B, SEQ, D = x_prompt.shape
    assert SEQ == 4 * OWN and x_sample.shape[1] == NS and x_sample.shape[0] == 16 and B == 2
    cache_ckv = f(inp["cache_ckv"])[0]; cache_kr = f(inp["cache_krope"])[0]; state_conv = f(inp["state_conv"])[0]
    KVR = cache_ckv.shape[2]
    assert cache_ckv.shape[1] == PAST
    w_in = f(inp["w_in"])[0]; b_in = f(inp["b_in"])[0]
    QR = inp["q_a_g"].shape[1]
    w_q_b = f(inp["w_q_b"])[0]
    H = w_q_b.shape[1] // 192
    DFF = inp["w_up"].shape[2]
    CD = D
    cfg = dict(D=D, QR=QR, KVR=KVR, H=H, DFF=DFF, ALPHA=float(2.0 ** 0.25))
    KB = D // 128
    o = 0
    wga = w_in[:, o:o + CD]; bga = b_in[o:o + CD]; o += CD
    wgb = w_in[:, o:o + CD]; bgb = b_in[o:o + CD]; o += CD
    wql = w_in[:, o:o + QR]; bql = b_in[o:o + QR]; o += QR
    wkl = w_in[:, o:o + KVR + ROPE]; bkl = b_in[o:o + KVR + ROPE]; o += KVR + ROPE
    wgate = w_in[:, o:o + 2 * D]; bgate = b_in[o:o + 2 * D]
    w_glu = np.concatenate([wga.reshape(D, CD // 256, 256), wgb.reshape(D, CD // 256, 256)], axis=2).reshape(D, 2 * CD)
    pm = lambda v: np.ascontiguousarray(v.reshape(-1, 128).T)
    wq3 = w_q_b.reshape(QR, H, 192)
    w_q = np.concatenate([wq3[:, :, 0:128], wq3[:, :, 128:192], wq3[:, :, 160:192], wq3[:, :, 128:160]], axis=2).reshape(QR, H * 256)
    half = ROPE // 2
    inv = (10000.0 ** (-np.arange(half, dtype=np.float32) / half)).astype(np.float32)

    def tables(pos):
        ang = pos.astype(np.float32)[:, None] * inv[None, :]
        return np.cos(ang).astype(np.float32), np.sin(ang).astype(np.float32)

    shared = dict(
        w_glu=np.ascontiguousarray(w_glu), w_ql=np.ascontiguousarray(wql), w_kl=np.ascontiguousarray(wkl),
        w_gate=np.ascontiguousarray(wgate),
        b_ga=pm(bga), b_gb=pm(bgb), b_ql=pm(bql), b_gate=pm(bgate), b_kl=np.ascontiguousarray(bkl[None, :]),
        wdw=np.ascontiguousarray(f(inp["w_dw"])[0].T.reshape(CD // 128, 128, CW).transpose(1, 0, 2).reshape(128, -1)),
        bdw=pm(f(inp["b_dw"])[0]), cng=pm(f(inp["conv_ln_g"])[0]), cnb=pm(f(inp["conv_ln_b"])[0]),
        w_pw=f(inp["w_conv_pw"])[0], qag=pm(f(inp["q_a_g"])[0]), w_q=np.ascontiguousarray(w_q),
        kvg=f(inp["kv_a_g"]), w_kv=f(inp["w_kv_b"])[0], w_o=f(inp["w_attn_o"])[0], w_out=f(inp["w_out"])[0],
        ln1g=f(inp["ln1_g"]), ln1b=f(inp["ln1_b"]), w_up=f(inp["w_up"])[0], w_down=f(inp["w_down"])[0],
        ln2g=f(inp["ln2_g"]), ln2b=f(inp["ln2_b"]),
    )
    in_maps = []
    for c in range(8):
        b = c // 4; j = c % 4; s0 = j * OWN
        m = dict(shared)
        m["x_all"] = np.concatenate([x_prompt[b, s0:s0 + OWN], x_sample[2 * c], x_sample[2 * c + 1]], axis=0)
        m["x_halo"] = x_prompt[b, s0 - 32:s0].copy() if j > 0 else np.zeros((32, D), np.float32)
        m["x_ctx"] = x_prompt[b, 0:NCTX]
        m["cache_ckv"] = np.concatenate([cache_ckv[2 * c], cache_ckv[2 * c + 1]], axis=0)
        m["cache_kr"] = np.concatenate([cache_kr[2 * c], cache_kr[2 * c + 1]], axis=0)
        sc = np.zeros((64, CD), np.float32)
        sc[0:30] = state_conv[2 * c]; sc[32:62] = state_conv[2 * c + 1]
        m["state_conv"] = sc
        m["halo_valid"] = np.full((128, 1), 1.0 if j > 0 else 0.0, np.float32)
        kbv = np.where(np.arange(NCTX) < s0, 0.0, -30000.0).astype(np.float32)
        m["keybias"] = np.ascontiguousarray(kbv.reshape(-1, 128).T)
        pos = np.concatenate([s0 + np.arange(OWN), PAST + np.arange(NS), PAST + np.arange(NS), np.arange(NCTX)])
        co, si = tables(pos)
        m["tm_cos"] = co; m["tm_sin"] = si
        m["fm_c"] = np.ascontiguousarray(np.concatenate([co[:T], co[:T]], axis=1).T)
        m["fm_s"] = np.ascontiguousarray(np.concatenate([-si[:T], si[:T]], axis=1).T)
        in_maps.append(m)
    return cfg, in_maps


def _assemble(cfg, res, B=2, SEQ=4 * OWN):
    D = cfg["D"]; KVR = cfg["KVR"]
    y_p = np.zeros((B, SEQ, D), np.float32); y_s = np.zeros((16, NS, D), np.float32)
    ckv_p = np.zeros((1, B, SEQ, KVR), np.float32); kr_p = np.zeros((1, B, SEQ, ROPE), np.float32)
    cs_p = np.zeros((1, B, CW - 1, D), np.float32)
    ckv_s = np.zeros((1, 16, NS, KVR), np.float32); kr_s = np.zeros((1, 16, NS, ROPE), np.float32)
    cs_s = np.zeros((1, 16, CW - 1, D), np.float32)
    for c in range(8):
        r = res[c]; b = c // 4; j = c % 4; s0 = j * OWN
        y_p[b, s0:s0 + OWN] = r["y_out"][0:OWN]
        ckv_p[0, b, s0:s0 + OWN] = r["ckv_out"][0:OWN]; kr_p[0, b, s0:s0 + OWN] = r["kr_out"][0:OWN]
        if j == 3:
            cs_p[0, b] = r["cs_out"][0:30]
        for s in range(2):
            y_s[2 * c + s] = r["y_out"][OWN + s * NS:OWN + (s + 1) * NS]
            ckv_s[0, 2 * c + s] = r["ckv_out"][OWN + s * NS:OWN + (s + 1) * NS]
            kr_s[0, 2 * c + s] = r["kr_out"][OWN + s * NS:OWN + (s + 1) * NS]
            cs_s[0, 2 * c + s] = r["cs_out"][32 + 32 * s:62 + 32 * s]
    return (y_p, y_s, ckv_p, kr_p, cs_p, ckv_s, kr_s, cs_s)


def kernel(_stages=None, _debug=False, _extra=None, **inputs):
    cfg, in_maps = _prep_inputs(inputs)
    cfg.update(_extra or {})
    if "maxops" in cfg:
        Sched.maxops = cfg["maxops"]; Sched.nrec = 0; Sched.log = []
    if _stages is not None:
        cfg["stages"] = _stages
        nc = build(cfg, debug=_debug)
        ncore = cfg.get("ncore", 8)
        res = run_bass_kernel_spmd(nc, in_maps[:ncore], core_ids=list(range(ncore)))
        return cfg, in_maps, res.results
    key = tuple(sorted(cfg.items()))
    if key not in _CACHE:
        _CACHE[key] = build(cfg)
    nc = _CACHE[key]
    res = run_bass_kernel_spmd(nc, in_maps, core_ids=list(range(8)))
    return _assemble(cfg, res.results)
```
